# Optimizing a Trainium2 kernel written in Bass

```python
import math
import jax, jax.numpy as jnp
from jax import lax
import numpy as np

D_MODEL = 1024
BATCH = 8
SEQ = 4096
DEPTH = 2

MEM_LEN = 256
BLOCK = 128
DIFF_HEADS = 8
DIFF_DK = 64
DIFF_DV = 2 * DIFF_DK
SWA_HEADS = 8
SWA_KV_HEADS = 2
SWA_HD = 64
WINDOW = 128
MEM_HEADS = 4
MEM_HD = 128
N_BRANCH = 3
D_FF = 2816
NEG_INF = -1e30
EPS = 1e-6

DIFF_QK_W = DIFF_HEADS * 2 * DIFF_DK
DIFF_V_W = DIFF_HEADS * DIFF_DV
SWA_Q_W = SWA_HEADS * SWA_HD
SWA_KV_W = SWA_KV_HEADS * SWA_HD
MEM_Q_W = MEM_HEADS * MEM_HD
GATE_W = N_BRANCH * D_MODEL
IN_SIZES = [DIFF_QK_W, DIFF_QK_W, DIFF_V_W, SWA_Q_W, SWA_KV_W, SWA_KV_W, MEM_Q_W, GATE_W]
IN_SPLITS = [int(v) for v in np.cumsum(IN_SIZES)[:-1]]
IN_W = int(sum(IN_SIZES))

kernel_name = "hybrid_gated_diffattn_swa_mem_macaron"


def rms_norm(x, g):
    xf = x.astype(jnp.float32)
    y = xf * lax.rsqrt(jnp.mean(xf * xf, axis=-1, keepdims=True) + EPS)
    return (y * g.astype(jnp.float32)).astype(x.dtype)


def swiglu(x, wi, wo):
    a, b = jnp.split(x @ wi, 2, axis=-1)
    return (jax.nn.silu(a) * b) @ wo


def alibi_slopes(n):
    return jnp.asarray([2.0 ** (-8.0 * (i + 1) / n) for i in range(n)], dtype=jnp.float32)


def diff_attention(q, k, v, lam, slopes):
    B, S, H = q.shape[0], q.shape[1], q.shape[2]
    nb = S // BLOCK
    qb = q.reshape(B, nb, BLOCK, H, 2, DIFF_DK).transpose(1, 0, 3, 4, 2, 5)
    kt = k.transpose(0, 2, 3, 1, 4)
    vt = v.transpose(0, 2, 1, 3)
    pos_k = jnp.arange(S)
    scale = DIFF_DK ** -0.5

    def one_block(args):
        qblk, n = args
        s = jnp.einsum('bhmqd,bhmkd->bhmqk', qblk, kt).astype(jnp.float32) * scale
        dist = n * BLOCK + jnp.arange(BLOCK)[:, None] - pos_k[None, :]
        logits = s - slopes[None, :, None, None, None] * dist.astype(jnp.float32)
        logits = jnp.where(dist >= 0, logits, NEG_INF)
        p = jax.nn.softmax(logits, axis=-1)
        pd = p[:, :, 0] - lam * p[:, :, 1]
        return jnp.einsum('bhqk,bhkd->bhqd', pd.astype(v.dtype), vt)

    out = lax.map(one_block, (qb, jnp.arange(nb)))
    return out.transpose(1, 0, 3, 2, 4).reshape(B, S, H, DIFF_DV)


def swa_attention(q, k, v, sinks, slopes):
    B, S = q.shape[0], q.shape[1]
    nb = S // BLOCK
    G = SWA_HEADS // SWA_KV_HEADS
    qb = q.reshape(B, nb, BLOCK, SWA_KV_HEADS, G, SWA_HD)
    kb = k.reshape(B, nb, BLOCK, SWA_KV_HEADS, SWA_HD)
    vb = v.reshape(B, nb, BLOCK, SWA_KV_HEADS, SWA_HD)

    def with_prev(t):
        prev = jnp.concatenate([jnp.zeros_like(t[:, :1]), t[:, :-1]], axis=1)
        return jnp.concatenate([prev, t], axis=2)

    kk, vv = with_prev(kb), with_prev(vb)
    s = jnp.einsum('bnqkgd,bnskd->bnkgqs', qb, kk).astype(jnp.float32) * (SWA_HD ** -0.5)
    qi = jnp.arange(BLOCK)
    sj = jnp.arange(2 * BLOCK)
    dist = qi[:, None] + BLOCK - sj[None, :]
    key_pos = jnp.arange(nb)[:, None] * BLOCK - BLOCK + sj[None, :]
    mask = ((dist >= 0) & (dist < WINDOW))[None] & (key_pos >= 0)[:, None, :]
    sl = slopes.reshape(SWA_KV_HEADS, G, 1, 1)
    logits = s - sl * dist.astype(jnp.float32)
    logits = jnp.where(mask[None, :, None, None], logits, NEG_INF)
    sink = jnp.broadcast_to(sinks.astype(jnp.float32).reshape(SWA_KV_HEADS, G, 1, 1),
                            logits.shape[:-1] + (1,))
    p = jax.nn.softmax(jnp.concatenate([logits, sink], axis=-1), axis=-1)[..., :-1]
    out = jnp.einsum('bnkgqs,bnskd->bnqkgd', p.astype(v.dtype), vv)
    return out.reshape(B, S, SWA_Q_W)


def memory_attention(q, mk, mv):
    B, S = q.shape[0], q.shape[1]
    s = jnp.einsum('bshd,bmhd->bhsm', q, mk).astype(jnp.float32) * (MEM_HD ** -0.5)
    p = jax.nn.softmax(s, axis=-1)
    out = jnp.einsum('bhsm,bmhd->bshd', p.astype(mv.dtype), mv)
    return out.reshape(B, S, MEM_Q_W)


def setup_inputs(seed: int = 0) -> dict:
    key = jax.random.key(seed)
    ks = jax.random.split(key, 24)
    L, D, F = DEPTH, D_MODEL, D_FF

    def w(k, shape, fan_in):
        return jax.random.normal(k, shape, jnp.float32) * (fan_in ** -0.5)

    def gain(k, shape):
        return 1.0 + 0.01 * jax.random.normal(k, shape, jnp.float32)

    return {
        "x": jax.random.normal(ks[0], (BATCH, SEQ, D), jnp.float32),
        "mem": jax.random.normal(ks[1], (BATCH, MEM_LEN, D), jnp.float32),
        "ffn1_norm": gain(ks[2], (L, D)),
        "ffn1_wi": w(ks[3], (L, D, 2 * F), D),
        "ffn1_wo": w(ks[4], (L, F, D), F),
        "mix_norm": gain(ks[5], (L, D)),
        "w_in": w(ks[6], (L, D, IN_W), D),
        "diff_lambda": 0.1 * jax.random.normal(ks[7], (L, 4, DIFF_DK), jnp.float32),
        "diff_subnorm": gain(ks[8], (L, DIFF_DV)),
        "swa_sinks": 0.5 * jax.random.normal(ks[9], (L, SWA_HEADS), jnp.float32),
        "mem_norm": gain(ks[10], (L, D)),
        "w_mem_kv": w(ks[11], (L, D, 2 * MEM_Q_W), D),
        "w_br_diff": w(ks[12], (L, DIFF_V_W, D), DIFF_V_W),
        "w_br_swa": w(ks[13], (L, SWA_Q_W, D), SWA_Q_W),
        "w_br_mem": w(ks[14], (L, MEM_Q_W, D), MEM_Q_W),
        "w_out": w(ks[15], (L, D, D), D),
        "ffn2_norm": gain(ks[16], (L, D)),
        "ffn2_wi": w(ks[17], (L, D, 2 * F), D),
        "ffn2_wo": w(ks[18], (L, F, D), F),
        "final_norm": gain(ks[19], (D,)),
    }


def reference(x, mem, ffn1_norm, ffn1_wi, ffn1_wo, mix_norm, w_in, diff_lambda, diff_subnorm,
              swa_sinks, mem_norm, w_mem_kv, w_br_diff, w_br_swa, w_br_mem, w_out,
              ffn2_norm, ffn2_wi, ffn2_wo, final_norm):
    B, S = x.shape[0], x.shape[1]
    diff_slopes = alibi_slopes(DIFF_HEADS)
    swa_slopes = alibi_slopes(SWA_HEADS)
    for l in range(DEPTH):
        h = x + 0.5 * swiglu(rms_norm(x, ffn1_norm[l]), ffn1_wi[l], ffn1_wo[l])
        u = rms_norm(h, mix_norm[l])
        q_d, k_d, v_d, q_s, k_s, v_s, q_m, gates = jnp.split(u @ w_in[l], IN_SPLITS, axis=-1)
        lambda_init = 0.8 - 0.6 * math.exp(-0.3 * l)
        lp = diff_lambda[l].astype(jnp.float32)
        lam = jnp.exp(jnp.sum(lp[0] * lp[1])) - jnp.exp(jnp.sum(lp[2] * lp[3])) + lambda_init
        o_d = diff_attention(q_d.reshape(B, S, DIFF_HEADS, 2, DIFF_DK),
                             k_d.reshape(B, S, DIFF_HEADS, 2, DIFF_DK),
                             v_d.reshape(B, S, DIFF_HEADS, DIFF_DV), lam, diff_slopes)
        o_d = (rms_norm(o_d, diff_subnorm[l]) * (1.0 - lambda_init)).reshape(B, S, DIFF_V_W)
        o_s = swa_attention(q_s.reshape(B, S, SWA_HEADS, SWA_HD),
                            k_s.reshape(B, S, SWA_KV_HEADS, SWA_HD),
                            v_s.reshape(B, S, SWA_KV_HEADS, SWA_HD),
                            swa_sinks[l], swa_slopes)
        mk, mv = jnp.split(rms_norm(mem, mem_norm[l]) @ w_mem_kv[l], 2, axis=-1)
        M = mem.shape[1]
        o_m = memory_attention(q_m.reshape(B, S, MEM_HEADS, MEM_HD),
                               mk.reshape(B, M, MEM_HEADS, MEM_HD),
                               mv.reshape(B, M, MEM_HEADS, MEM_HD))
        g = jax.nn.sigmoid(gates).reshape(B, S, N_BRANCH, D_MODEL)
        merged = (g[:, :, 0] * (o_d @ w_br_diff[l])
                  + g[:, :, 1] * (o_s @ w_br_swa[l])
                  + g[:, :, 2] * (o_m @ w_br_mem[l]))
        h = h + merged @ w_out[l]
        x = h + 0.5 * swiglu(rms_norm(h, ffn2_norm[l]), ffn2_wi[l], ffn2_wo[l])
    return rms_norm(x, final_norm)
```

```python
import numpy as np
from contextlib import ExitStack
import concourse.bass as bass
import concourse.mybir as mybir
from concourse.bass_utils import run_bass_kernel_spmd

F32 = mybir.dt.float32
BF16 = mybir.dt.bfloat16
ALU = mybir.AluOpType
AF = mybir.ActivationFunctionType

D = 1024
S = 4096
NB = 8
DEPTH = 2
MEM = 256
DFF = 2816
INW = 7424
EPS = 1e-6
T = 512
BIGB = 200 * 1024
NT = S // T
KC = D // 128
C_IDENT, C_TRI, C_SWAMB, C_ONESL, C_ONESR, C_ONES128, C_ONES1 = 0, 128, 256, 2304, 2432, 2560, 2688
NC16 = 2816
NEG = -30000.0
WIN_A = 4352


class Res:
    __slots__ = ("name", "writer", "readers")

    def __init__(self, name=""):
        self.name = name
        self.writer = None
        self.readers = []


class Op:
    __slots__ = ("eng", "fn", "deps", "dma", "tok", "needed", "idx")

    def __init__(self, eng, fn, dma):
        self.eng = eng
        self.fn = fn
        self.dma = dma
        self.deps = []
        self.tok = None
        self.needed = False
        self.idx = None


ENGS = ("pe", "act", "dve", "pool", "sp")


class Rec:
    def __init__(self, nc, stack):
        self.nc = nc
        self.stack = stack
        self.ops = {e: [] for e in ENGS}
        self.esem = {e: stack.enter_context(nc.semaphore("s_" + e)) for e in ENGS}
        self.dsem = {}
        self.nops = 0

    def _dma_sem(self, key):
        if key not in self.dsem:
            h = self.stack.enter_context(self.nc.semaphore("d_" + key))
            self.dsem[key] = [h, 0]
        return self.dsem[key]

    def op(self, eng, fn, reads=(), writes=(), dma=None):
        o = Op(eng, fn, dma)
        deps = {}
        for r in reads:
            w = r.writer
            if w is not None:
                deps[id(w)] = (w, True)
        for r in writes:
            w = r.writer
            if w is not None and id(w) not in deps:
                deps[id(w)] = (w, True)
            for rd in r.readers:
                if id(rd) not in deps:
                    deps[id(rd)] = (rd, False)
        for d, strong in deps.values():
            if d is o:
                continue
            if d.dma is None and dma is None and d.eng == eng:
                if eng == "pe" or not strong:
                    continue
            o.deps.append(d)
            d.needed = True
        for r in reads:
            r.readers.append(o)
        for r in writes:
            r.writer = o
            r.readers = []
        if dma is not None:
            s = self._dma_sem(dma)
            s[1] += 16
            o.tok = (s[0], s[1])
        self.ops[eng].append(o)
        self.nops += 1
        return o

    def barrier(self, exclude=()):
        last = []
        for e in ENGS:
            for o in reversed(self.ops[e]):
                if o.dma is None and o.fn is not None:
                    last.append(o)
                    break
        dtoks = [(s[0], s[1]) for k, s in self.dsem.items() if s[1] > 0 and k not in exclude]
        for e in ENGS:
            o = Op(e, None, None)
            for d in last:
                if d.eng != e:
                    o.deps.append(d)
                    d.needed = True
            o.tok = None
            o.idx = dtoks
            self.ops[e].append(o)

    def emit(self):
        nc = self.nc
        for e in ENGS:
            n = 0
            for o in self.ops[e]:
                if o.dma is None and o.needed:
                    n += 1
                    o.tok = (self.esem[e], n)
        with nc.Block() as block:
            def run(ename):
                def body(eng):
                    waited = {}
                    for o in self.ops[ename]:
                        want = {}
                        for d in o.deps:
                            sem, val = d.tok
                            k = id(sem)
                            if k not in want or want[k][1] < val:
                                want[k] = (sem, val)
                        if o.fn is None and isinstance(o.idx, list):
                            for sem, val in o.idx:
                                k = id(sem)
                                if k not in want or want[k][1] < val:
                                    want[k] = (sem, val)
                        for k, (sem, val) in want.items():
                            if waited.get(k, 0) >= val:
                                continue
                            eng.wait_ge(sem, val)
                            waited[k] = val
                        if o.fn is None:
                            continue
                        ins = o.fn(eng)
                        if o.dma is not None:
                            ins.then_inc(o.tok[0], 16)
                        elif o.needed:
                            ins.then_inc(o.tok[0], 1)
                return body
            block.tensor(run("pe"))
            block.scalar(run("act"))
            block.vector(run("dve"))
            block.gpsimd(run("pool"))
            block.sync(run("sp"))


def vec_cols():
    cols = {}
    n = 0
    for l in range(DEPTH):
        for name in ("ffn1_norm", "mix_norm", "ffn2_norm", "mem_norm"):
            cols[(name, l)] = n
            n += KC
        cols[("diff_subnorm", l)] = n
        n += 1
    cols[("final_norm", 0)] = n
    n += KC
    return cols, n


VCOLS, NV = vec_cols()


class Ctx:
    pass


class Alloc:
    def __init__(self, big):
        self.big = big
        self.off = 0

    def take(self, n, dt):
        nbytes = n * (4 if dt == F32 else 2)
        assert self.off % 4 == 0
        a = self.big[:, self.off // 2:(self.off + nbytes) // 2]
        self.off += nbytes
        assert self.off <= BIGB, self.off
        return a.bitcast(F32) if dt == F32 else a


def build(n_phases=100, debug_out=None, stop_after=""):
    nc = bass.Bass("TRN2", target_bir_lowering=False)
    stack = ExitStack()
    with stack:
        _build(nc, stack, n_phases, debug_out, stop_after)
    return nc


def _build(nc, stack, n_phases, debug_out, stop_after=""):
    rec = Rec(nc, stack)
    g = Ctx()
    g.stop_after = stop_after
    g.nc, g.rec = nc, rec

    def din(name, shape, dt=F32):
        return nc.dram_tensor(name, list(shape), dt, kind="ExternalInput").ap()

    g.x = din("x", [S, D])
    g.mem = din("mem", [MEM, D])
    g.ffn_wi = [din("ffn1_wi", [DEPTH, D, 2 * DFF]), din("ffn2_wi", [DEPTH, D, 2 * DFF])]
    g.ffn_wo = [din("ffn1_wo", [DEPTH, DFF, D]), din("ffn2_wo", [DEPTH, DFF, D])]
    g.w_in = din("w_in", [DEPTH, D, INW])
    g.w_mem_kv = din("w_mem_kv", [DEPTH, D, D])
    g.w_br_diff = din("w_br_diff", [DEPTH, D, D])
    g.w_br_swa = din("w_br_swa", [DEPTH, 512, D])
    g.w_br_mem = din("w_br_mem", [DEPTH, 512, D])
    g.w_out = din("w_out", [DEPTH, D, D])
    g.vecs_d = din("vecs", [128, NV])
    g.ident_d = din("ident", [128, 128])
    g.cf32_d = din("cf32", [128, NC16])
    g.qaug_d = din("qaug", [2, T])
    g.btab_d = din("btab", [128, 256])
    g.dlam_d = din("diff_lambda", [DEPTH, 256])
    g.sinks_d = din("swa_sinks", [DEPTH, 8])
    g.out = nc.dram_tensor("out", [S, D], F32, kind="ExternalOutput").ap()

    def scratch(name, shape, dt):
        kind = "ExternalOutput" if (debug_out and name in debug_out) else "Internal"
        return nc.dram_tensor(name, list(shape), dt, kind=kind).ap()

    g.R = scratch("R", [KC, 128, S], F32)
    g.XN = scratch("XN", [KC, 128, S], BF16)
    g.OD = scratch("OD", [8, 128, S], BF16)
    g.OS = scratch("OS", [4, 128, S], BF16)
    g.OM = scratch("OM", [4, 128, S], BF16)

    def sb(name, shape, dt):
        return stack.enter_context(nc.sbuf_tensor(name, list(shape), dt))

    g.vecs = sb("vecs_sb", [128, NV], F32)
    g.ident = sb("ident_sb", [128, 128], F32)
    g.onesD = sb("onesD", [128, 128], BF16)
    g.BIG = sb("BIG", [128, BIGB // 2], BF16)
    g.r_R = [Res("R%d" % t) for t in range(NT)]
    g.r_XN = [Res("XN%d" % t) for t in range(NT)]
    g.PS = stack.enter_context(nc.psum_tensor("PS", [128, 4096], F32))
    g.psum = [g.PS[:, i * 512:(i + 1) * 512] for i in range(8)]
    g.psres = [Res("ps%d" % i) for i in range(8)]

    r_const = Res("const")
    rec.op("sp", lambda e: e.dma_start(out=g.vecs[:], in_=g.vecs_d[:, :]), writes=[r_const], dma="const")
    rec.op("sp", lambda e: e.dma_start(out=g.ident[:], in_=g.ident_d[:, :]), writes=[r_const], dma="const")
    rec.op("dve", lambda e: e.memset(g.onesD[:], 1.0 / D), writes=[r_const])
    g.epsc = sb("epsc", [128, 1], F32)
    rec.op("dve", lambda e: e.memset(g.epsc[:], EPS), writes=[r_const])
    g.cb = sb("cb", [128, NC16], BF16)
    for i in range(0, NC16, 1408):
        rec.op("pool", lambda e, i=i: e.dma_start(out=g.cb[:, i:i + 1408], in_=g.cf32_d[:, i:i + 1408]),
               writes=[r_const], dma="const2")
    g.btab = sb("btab_sb", [128, 256], F32)
    rec.op("sp", lambda e: e.dma_start(out=g.btab[:], in_=g.btab_d[:, :]), writes=[r_const], dma="const")
    g.identb = g.cb[:, C_IDENT:C_IDENT + 128]
    g.tri = g.cb[:, C_TRI:C_TRI + 128]
    g.onesL = g.cb[:, C_ONESL:C_ONESL + 128]
    g.onesR = g.cb[:, C_ONESR:C_ONESR + 128]
    g.ones128 = g.cb[:, C_ONES128:C_ONES128 + 128]
    g.ones1 = g.cb[:, C_ONES1:C_ONES1 + 128]
    g.small = sb("small", [128, 32], F32)
    g.r_const = r_const
    rec.barrier()

    phases = []
    phases.append(lambda: phase_t0(g))
    for l in range(DEPTH):
        phases.append(lambda l=l: phase_ffn(g, l, 0))
        phases.append(lambda l=l: phase_m1(g, l))
        phases.append(lambda l=l: phase_m2(g, l))
        phases.append(lambda l=l: phase_ffn(g, l, 1))
    phases.append(lambda: phase_final(g))
    g.pref = {}
    g.bar_excl = ()
    for i, p in enumerate(phases):
        if i >= n_phases:
            break
        p()
        rec.barrier(exclude=g.bar_excl)
    rec.emit()


def phase_t0(g):
    nc, rec = g.nc, g.rec
    al = Alloc(g.BIG)
    xin = [al.take(4096, F32) for i in range(2)]
    rt = [al.take(4096, F32) for i in range(2)]
    r_xin = [Res("xin0"), Res("xin1")]
    r_rt = [Res("rt0"), Res("rt1")]
    Rv = g.R.rearrange("c p n -> p c n")
    for t in range(NT):
        b = t % 2
        src = g.x[t * T:(t + 1) * T, :].rearrange("(s p) d -> p s d", p=128)
        dst = xin[b].rearrange("p (s d) -> p s d", s=4)
        rec.op("sp", lambda e, dst=dst, src=src: e.dma_start(out=dst, in_=src),
               writes=[r_xin[b]], dma="xin%d" % b)
        for c in range(KC):
            pb = c % 4
            ps = g.psum[pb]

            def tr(e, b=b, c=c, ps=ps):
                ins = None
                for s in range(4):
                    ins = e.transpose(out=ps[:, s * 128:(s + 1) * 128],
                                      in_=xin[b][:, s * 1024 + c * 128: s * 1024 + (c + 1) * 128],
                                      identity=g.ident[:])
                return ins
            rec.op("pe", tr, reads=[r_xin[b], g.r_const], writes=[g.psres[pb]])
            dsl = rt[b][:, c * 512:(c + 1) * 512]
            if c % 2 == 0:
                rec.op("act", lambda e, dsl=dsl, ps=ps: e.copy(out=dsl, in_=ps[:]),
                       reads=[g.psres[pb]], writes=[r_rt[b]])
            else:
                rec.op("dve", lambda e, dsl=dsl, ps=ps: e.tensor_copy(out=dsl, in_=ps[:]),
                       reads=[g.psres[pb]], writes=[r_rt[b]])
        src = rt[b].rearrange("p (c n) -> p c n", c=KC)
        rec.op("sp", lambda e, src=src, t=t: e.dma_start(out=Rv[:, :, t * T:(t + 1) * T], in_=src),
               reads=[r_rt[b]], writes=[g.r_R[t]], dma="rtst%d" % b)


def rsqrt_ps(g, ps, r_ps, rstd, r_rstd):
    rec = g.rec
    rec.op("act", lambda e: e.activation(out=rstd, in_=ps[:], func=AF.Ln, bias=g.epsc[:, 0:1]),
           reads=[r_ps, g.r_const], writes=[r_rstd])
    rec.op("act", lambda e: e.activation(out=rstd, in_=rstd, func=AF.Exp, scale=-0.5),
           reads=[r_rstd], writes=[r_rstd])


def staged_load(g, pieces, stages, r_dst_default=None):
    rec = g.rec
    ns = len(stages)
    r_st = [Res() for _ in range(ns)]
    for i, pc in enumerate(pieces):
        dst, src = pc[0], pc[1]
        r_dst = pc[2] if len(pc) > 2 else r_dst_default
        k = i % ns
        n = 1
        for d in src.shape[1:]:
            n *= d
        assert n <= 4096
        st = stages[k][:, 0:n]
        st_in = st.rearrange("p (a b) -> p a b", a=src.shape[1]) if len(src.shape) == 3 else st
        rec.op("sp", lambda e, st_in=st_in, src=src: e.dma_start(out=st_in, in_=src), writes=[r_st[k]],
               dma="stg%d" % k)
        eng = ("dve", "pool", "act")[i % 3]
        if eng == "act":
            rec.op(eng, lambda e, dst=dst, st=st: e.copy(out=dst, in_=st), reads=[r_st[k]], writes=[r_dst])
        else:
            rec.op(eng, lambda e, dst=dst, st=st: e.tensor_copy(out=dst, in_=st), reads=[r_st[k]], writes=[r_dst])


def norm_tile(g, rt, r_rt, sq, r_sq, rstd, r_rstd, ps_i, xn, r_xn, gcol, xn_stride=T, part=None):
    rec = g.rec
    if part in (None, "a"):
        rec.op("act", lambda e: e.activation(out=sq, in_=rt, func=AF.Square),
               reads=[r_rt], writes=[r_sq])
    if part == "a":
        return
    ps = g.psum[ps_i]

    def mm(e):
        ins = None
        for c in range(KC):
            ins = e.matmul(ps[:], lhsT=g.onesD[:], rhs=sq[:, c * T:(c + 1) * T],
                           start=(c == 0), stop=(c == KC - 1))
        return ins
    rec.op("pe", mm, reads=[r_sq, g.r_const], writes=[g.psres[ps_i]])
    rsqrt_ps(g, ps, g.psres[ps_i], rstd, r_rstd)
    for c in range(KC):
        rec.op("dve", lambda e, c=c: e.scalar_tensor_tensor(
            out=xn[:, c * xn_stride:c * xn_stride + T], in0=rt[:, c * T:(c + 1) * T],
            scalar=g.vecs[:, gcol + c:gcol + c + 1], in1=rstd, op0=ALU.mult, op1=ALU.mult),
            reads=[r_rt, r_rstd, g.r_const], writes=[r_xn])


def phase_ffn(g, l, which):
    nc, rec = g.nc, g.rec
    al = Alloc(g.BIG)
    HJ = 11
    HW = HJ * 128
    wi_sb, wo_sb = [None, None], [None, None]
    wi_sb[0] = al.take(KC * 2 * HW, BF16)
    wo_sb[0] = al.take(HJ * D, BF16)
    al.off = 69632
    wi_sb[1] = al.take(KC * 2 * HW, BF16)
    wo_sb[1] = al.take(HJ * D, BF16)
    woff = al.off
    stages = [al.take(4096, F32) for i in range(3)]
    al.off = woff
    rt = [al.take(KC * T, F32) for i in range(2)]
    rstd = al.take(T, F32)
    sa = [al.take(T, F32) for i in range(2)]
    xn = al.take(KC * T, BF16)
    hid = al.take(HJ * T, BF16)
    sq = al.take(KC * T, BF16)
    r_w = [Res("w0"), Res("w1")]
    r_rt = [Res("rt0"), Res("rt1")]
    r_rstd, r_xn, r_hid, r_sq = Res("rstd"), Res("xn"), Res("hid"), Res("sq")
    r_sa = [Res("sa0"), Res("sa1")]
    wi_d = g.ffn_wi[which][l].rearrange("(c p) n -> p c n", p=128)
    wo_d = g.ffn_wo[which][l]
    gcol = VCOLS[("ffn1_norm" if which == 0 else "ffn2_norm", l)]
    Rv = g.R.rearrange("c p n -> p c n")
    XNv = g.XN.rearrange("c p n -> p c n")

    for h in (1,):
        wv = wi_sb[h].rearrange("p (c n) -> p c n", c=KC)
        for part in range(2):
            for c in range(KC):
                src = wi_d[:, c, part * DFF + h * HW: part * DFF + (h + 1) * HW]
                dst = wv[:, c, part * HW:(part + 1) * HW]
                rec.op("pool", lambda e, dst=dst, src=src: e.dma_start(out=dst, in_=src),
                       writes=[r_w[h]], dma="w%d" % h)
        src = wo_d[h * HW:(h + 1) * HW, :].rearrange("(j p) n -> p j n", p=128)
        dst = wo_sb[h].rearrange("p (j n) -> p j n", j=HJ)
        rec.op("pool", lambda e, dst=dst, src=src: e.dma_start(out=dst, in_=src),
               writes=[r_w[h]], dma="w%d" % h)
    pieces = []
    for c in range(KC):
        src = wi_d[:, c, :].rearrange("p (two n) -> p two n", two=2)[:, :, 0:HW]
        pieces.append((wi_sb[0][:, c * 2 * HW:(c + 1) * 2 * HW], src))
    wo3 = wo_d[0:HW, :].rearrange("(j p) n -> p j n", p=128)
    for j0 in range(0, HJ, 4):
        j1 = min(HJ, j0 + 4)
        pieces.append((wo_sb[0][:, j0 * D:j1 * D], wo3[:, j0:j1, :]))
    pk = ("ffn0", l, which)
    if pk in g.pref:
        r_w[0] = g.pref.pop(pk)
    else:
        staged_load(g, pieces, stages, r_w[0])
        rec.barrier(exclude=("w1", "pf"))
    g.bar_excl = ()

    def prefetch_next():
        if which == 0:
            r_new = Res("pf_win")
            wvn = g.w_in[l].rearrange("(c p) n -> p c n", p=128)
            for c in range(KC):
                for i in range(0, WIN_A, 1088):
                    rec.op("pool", lambda e, c=c, i=i: e.dma_start(
                        out=g.BIG[:, c * WIN_A + i: c * WIN_A + i + 1088], in_=wvn[:, c, i:i + 1088]),
                        writes=[r_w[0], r_new], dma="pf")
            g.pref[("win", l)] = r_new
            g.bar_excl = ("pf",)
        elif l + 1 < DEPTH:
            r_new = Res("pf_ffn")
            wi_n = g.ffn_wi[0][l + 1].rearrange("(c p) n -> p c n", p=128)
            wo_n = g.ffn_wo[0][l + 1]
            wvv = wi_sb[0].rearrange("p (c n) -> p c n", c=KC)
            for part in range(2):
                for c in range(KC):
                    src = wi_n[:, c, part * DFF: part * DFF + HW]
                    dst = wvv[:, c, part * HW:(part + 1) * HW]
                    rec.op("pool", lambda e, dst=dst, src=src: e.dma_start(out=dst, in_=src),
                           writes=[r_w[0], r_new], dma="pf")
            src = wo_n[0:HW, :].rearrange("(j p) n -> p j n", p=128)
            dst = wo_sb[0].rearrange("p (j n) -> p j n", j=HJ)
            rec.op("pool", lambda e, dst=dst, src=src: e.dma_start(out=dst, in_=src),
                   writes=[r_w[0], r_new], dma="pf")
            g.pref[("ffn0", l + 1, 0)] = r_new
            g.bar_excl = ("pf",)

    def load_b(t, full=False):
        b = t % 2
        if full:
            norm_tile(g, rt[b], r_rt[b], sq, r_sq, rstd, r_rstd, 6, xn, r_xn, gcol, part="b")
        src = xn.rearrange("p (c n) -> p c n", c=KC)
        rec.op("sp", lambda e: e.dma_start(out=XNv[:, :, t * T:(t + 1) * T], in_=src),
               reads=[r_xn], writes=[g.r_XN[t]], dma="xnst")

    def load(h, t, split=False):
        b = t % 2
        dst = rt[b].rearrange("p (c n) -> p c n", c=KC)
        rec.op("sp", lambda e: e.dma_start(out=dst, in_=Rv[:, :, t * T:(t + 1) * T]),
               reads=[g.r_R[t]], writes=[r_rt[b]], dma="rtld%d" % b)
        if h == 0:
            norm_tile(g, rt[b], r_rt[b], sq, r_sq, rstd, r_rstd, 6, xn, r_xn, gcol, part=("a" if split else None))
            if not split:
                load_b(t)
        else:
            dstx = xn.rearrange("p (c n) -> p c n", c=KC)
            rec.op("sp", lambda e: e.dma_start(out=dstx, in_=XNv[:, :, t * T:(t + 1) * T]),
                   reads=[g.r_XN[t]], writes=[r_xn], dma="xnld")

    for h in range(2):
        if h == 1:
            prefetch_next()
        load(h, 0)
        for t in range(NT):
            b = t % 2
            for j in range(HJ):
                for part in range(2):
                    pi = part * 2 + j % 2
                    ps = g.psum[pi]

                    def mm(e, ps=ps, part=part, j=j, h=h):
                        ins = None
                        for c in range(KC):
                            o0 = c * 2 * HW + part * HW + j * 128
                            ins = e.matmul(ps[:], lhsT=wi_sb[h][:, o0:o0 + 128], rhs=xn[:, c * T:(c + 1) * T],
                                           start=(c == 0), stop=(c == KC - 1))
                        return ins
                    rec.op("pe", mm, reads=[r_w[h], r_xn], writes=[g.psres[pi]])
                psa, psb = g.psum[j % 2], g.psum[2 + j % 2]
                sj = sa[j % 2]
                rec.op("act", lambda e, psa=psa, sj=sj: e.activation(out=sj, in_=psa[:], func=AF.Silu),
                       reads=[g.psres[j % 2]], writes=[r_sa[j % 2]])
                hj = hid[:, j * T:(j + 1) * T]
                rec.op("dve", lambda e, hj=hj, sj=sj, psb=psb: e.tensor_tensor(out=hj, in0=sj, in1=psb[:], op=ALU.mult),
                       reads=[r_sa[j % 2], g.psres[2 + j % 2]], writes=[r_hid])
            if t + 1 < NT:
                load(h, t + 1, split=True)
            for oc in range(KC):
                if oc == 4 and h == 0 and t + 1 < NT:
                    load_b(t + 1, full=True)
                pi = 4 + oc % 2
                ps = g.psum[pi]

                def mm2(e, ps=ps, oc=oc, h=h):
                    ins = None
                    for j in range(HJ):
                        ins = e.matmul(ps[:], lhsT=wo_sb[h][:, j * D + oc * 128: j * D + (oc + 1) * 128],
                                       rhs=hid[:, j * T:(j + 1) * T], start=(j == 0), stop=(j == HJ - 1))
                    return ins
                rec.op("pe", mm2, reads=[r_w[h], r_hid], writes=[g.psres[pi]])
                ro = rt[b][:, oc * T:(oc + 1) * T]
                rec.op("dve", lambda e, ro=ro, ps=ps: e.scalar_tensor_tensor(
                    out=ro, in0=ps[:], scalar=0.5, in1=ro, op0=ALU.mult, op1=ALU.add),
                    reads=[g.psres[pi], r_rt[b]], writes=[r_rt[b]])
            src = rt[b].rearrange("p (c n) -> p c n", c=KC)
            rec.op("sp", lambda e, src=src, t=t: e.dma_start(out=Rv[:, :, t * T:(t + 1) * T], in_=src),
                   reads=[r_rt[b]], writes=[g.r_R[t]], dma="rtst%d" % b)


def mm_group(ps, pairs):
    def fn(e):
        ins = None
        n = len(pairs)
        for i, (l, r) in enumerate(pairs):
            ins = e.matmul(ps, lhsT=l, rhs=r, start=(i == 0), stop=(i == n - 1))
        return ins
    return fn


def phase_m1(g, l):
    import math
    nc, rec = g.nc, g.rec
    al = Alloc(g.BIG)
    win_sb = al.take(KC * WIN_A, BF16)
    uT = al.take(KC * S, BF16)
    g.mkT = al.take(4 * MEM, BF16)
    g.mv = al.take(2 * 512, BF16)
    xoff = al.off
    r_win, r_uT, r_sm = Res("win"), Res("uT"), Res("sm")
    sm = g.small
    lp = uT[:, 0:1024].bitcast(F32)
    stages = [al.take(4096, F32) for i in range(3)]
    wmem_sb = al.take(KC * D, BF16)
    al.off = xoff
    lam_init = 0.8 - 0.6 * math.exp(-0.3 * l)
    Rv = g.R.rearrange("c p n -> p c n")
    XNv = g.XN.rearrange("c p n -> p c n")

    def W(c, col, n=128):
        return win_sb[:, c * WIN_A + col: c * WIN_A + col + n]

    def U(c, tok, n):
        return uT[:, c * S + tok: c * S + tok + n]

    wv = g.w_in[l].rearrange("(c p) n -> p c n", p=128)
    wmv = g.w_mem_kv[l].rearrange("(c p) n -> p c n", p=128)
    r_wmem = Res("wmem")
    pieces = []
    if ("win", l) in g.pref:
        r_win = g.pref.pop(("win", l))
    else:
        for c in range(KC):
            for i in (0, 2176):
                pieces.append((W(c, i, 2176), wv[:, c, i:i + 2176]))
    g.bar_excl = ()
    pieces.append((wmem_sb[:, 0:4096], wmv[:, 0:4, :], r_wmem))
    pieces.append((wmem_sb[:, 4096:8192], wmv[:, 4:8, :], r_wmem))
    staged_load(g, pieces, stages, r_win)

    rec.op("sp", lambda e: e.dma_start(out=lp[:, 256:512], in_=g.dlam_d[l].partition_broadcast(128)),
           writes=[r_sm], dma="sm")
    rec.op("sp", lambda e: e.dma_start(out=sm[:, 8:16], in_=g.sinks_d[l].partition_broadcast(128)),
           writes=[r_sm], dma="sm")
    rec.op("dve", lambda e: e.tensor_tensor(out=lp[:, 32:96], in0=lp[:, 256:320], in1=lp[:, 320:384], op=ALU.mult),
           reads=[r_sm], writes=[r_sm])
    rec.op("dve", lambda e: e.tensor_tensor(out=lp[:, 96:160], in0=lp[:, 384:448], in1=lp[:, 448:512], op=ALU.mult),
           reads=[r_sm], writes=[r_sm])
    rec.op("dve", lambda e: e.reduce_sum(out=sm[:, 16:17], in_=lp[:, 32:96], axis=mybir.AxisListType.X),
           reads=[r_sm], writes=[r_sm])
    rec.op("dve", lambda e: e.reduce_sum(out=sm[:, 17:18], in_=lp[:, 96:160], axis=mybir.AxisListType.X),
           reads=[r_sm], writes=[r_sm])
    rec.op("act", lambda e: e.activation(out=sm[:, 18:20], in_=sm[:, 16:18], func=AF.Exp),
           reads=[r_sm], writes=[r_sm])
    rec.op("dve", lambda e: e.scalar_tensor_tensor(out=sm[:, 0:1], in0=sm[:, 19:20], scalar=-lam_init,
                                                   in1=sm[:, 18:19], op0=ALU.add, op1=ALU.subtract),
           reads=[r_sm], writes=[r_sm])
    sc = VCOLS[("diff_subnorm", l)]
    rec.op("dve", lambda e: e.tensor_scalar(out=sm[:, 1:2], in0=g.vecs[:, sc:sc + 1], scalar1=1.0 - lam_init,
                                            scalar2=None, op0=ALU.mult),
           reads=[r_sm, g.r_const], writes=[r_sm])
    sk = sm[:, 8:16].rearrange("p (a b) -> p a b", b=2)
    rec.op("act", lambda e: e.activation(out=sm[0:64, 2:6], in_=sk[0:64, :, 0], func=AF.Exp),
           reads=[r_sm], writes=[r_sm])
    rec.op("act", lambda e: e.activation(out=sm[64:128, 2:6], in_=sk[64:128, :, 1], func=AF.Exp),
           reads=[r_sm], writes=[r_sm])

    rec.barrier(exclude=("pf",))
    al.off = xoff
    rt = [al.take(KC * T, F32) for i in range(2)]
    sq = al.take(KC * T, BF16)
    rstd = [al.take(T, F32) for i in range(2)]
    r_rt, r_sq, r_rstd = [Res(), Res()], Res(), [Res(), Res()]
    gcol = VCOLS[("mix_norm", l)]
    for t in range(NT):
        b = t % 2
        rec.op("sp", lambda e, t=t, b=b: e.dma_start(out=rt[b].rearrange("p (c n) -> p c n", c=KC),
                                                     in_=Rv[:, :, t * T:(t + 1) * T]),
               reads=[g.r_R[t]], writes=[r_rt[b]], dma="rtld%d" % b)
        norm_tile(g, rt[b], r_rt[b], sq, r_sq, rstd[b], r_rstd[b], 4 + b, uT[:, t * T:], r_uT, gcol, xn_stride=S)
        src = uT.rearrange("p (c n) -> p c n", c=KC)[:, :, t * T:(t + 1) * T]
        rec.op("sp", lambda e, t=t, src=src: e.dma_start(out=XNv[:, :, t * T:(t + 1) * T], in_=src),
               reads=[r_uT], writes=[g.r_XN[t]], dma="xnst")
    rec.barrier(exclude=("pf",))
    al.off = xoff
    memx = al.take(2 * D, F32)
    memnT = al.take(KC * MEM, BF16)
    junk = al.take(D, BF16)
    r_memx, r_memnT, r_junk = Res(), Res(), Res()
    r_mk = Res()
    rec.op("sp", lambda e: e.dma_start(out=memx.rearrange("p (a d) -> p a d", a=2),
                                       in_=g.mem.rearrange("(a p) d -> p a d", p=128)),
           writes=[r_memx], dma="memx")
    for a in range(2):
        rec.op("act", lambda e, a=a: e.activation(out=junk, in_=memx[:, a * D:(a + 1) * D], func=AF.Square,
                                                  accum_out=sm[:, 20 + a:21 + a]),
               reads=[r_memx], writes=[r_junk, r_sm])
    rec.op("act", lambda e: e.activation(out=sm[:, 22:24], in_=sm[:, 20:22], func=AF.Ln, bias=g.epsc[:, 0:1],
                                         scale=1.0 / D),
           reads=[r_sm, g.r_const], writes=[r_sm])
    rec.op("act", lambda e: e.activation(out=sm[:, 22:24], in_=sm[:, 22:24], func=AF.Exp, scale=-0.5),
           reads=[r_sm], writes=[r_sm])
    for a in range(2):
        rec.op("dve", lambda e, a=a: e.tensor_scalar(out=memx[:, a * D:(a + 1) * D], in0=memx[:, a * D:(a + 1) * D],
                                                     scalar1=sm[:, 22 + a:23 + a], scalar2=None, op0=ALU.mult),
               reads=[r_memx, r_sm], writes=[r_memx])
    mcol = VCOLS[("mem_norm", l)]
    for c in range(KC):
        pi = 6 + c % 2
        ps = g.psum[pi]

        def tr(e, c=c, ps=ps):
            ins = None
            for a in range(2):
                ins = e.transpose(out=ps[:, a * 128:(a + 1) * 128],
                                  in_=memx[:, a * D + c * 128: a * D + (c + 1) * 128], identity=g.ident[:])
            return ins
        rec.op("pe", tr, reads=[r_memx, g.r_const], writes=[g.psres[pi]])
        rec.op("dve", lambda e, c=c, ps=ps: e.tensor_scalar(out=memnT[:, c * MEM:(c + 1) * MEM], in0=ps[:, 0:MEM],
                                                            scalar1=g.vecs[:, mcol + c:mcol + c + 1], scalar2=None,
                                                            op0=ALU.mult),
               reads=[g.psres[pi], g.r_const], writes=[r_memnT])
    for hm in range(4):
        pi = 6 + hm % 2
        ps = g.psum[pi]
        rec.op("pe", mm_group(ps[:, 0:MEM], [(wmem_sb[:, c * D + hm * 128: c * D + (hm + 1) * 128],
                                              memnT[:, c * MEM:(c + 1) * MEM]) for c in range(KC)]),
               reads=[r_wmem, r_memnT], writes=[g.psres[pi]])
        rec.op("dve", lambda e, hm=hm, ps=ps: e.tensor_copy(out=g.mkT[:, hm * MEM:(hm + 1) * MEM], in_=ps[:, 0:MEM]),
               reads=[g.psres[pi]], writes=[r_mk])
    for mc in range(2):
        pi = 6 + mc % 2
        ps = g.psum[pi]
        rec.op("pe", mm_group(ps[:, :], [(memnT[:, c * MEM + mc * 128: c * MEM + (mc + 1) * 128],
                                          wmem_sb[:, c * D + 512: c * D + 1024]) for c in range(KC)]),
               reads=[r_wmem, r_memnT], writes=[g.psres[pi]])
        rec.op("dve", lambda e, mc=mc, ps=ps: e.tensor_copy(out=g.mv[:, mc * 512:(mc + 1) * 512], in_=ps[:, :]),
               reads=[g.psres[pi]], writes=[r_mk])

    rec.barrier()
    if getattr(g, "stop_after", "") == "m1u":
        return

    al.off = xoff
    QZe = [al.take(T, BF16) for i in range(4)]
    QZo = [al.take(T, BF16) for i in range(4)]
    QM = al.take(4 * T, BF16)
    KD = [al.take(640, BF16) for i in range(2)]
    VL = [al.take(640, BF16) for i in range(2)]
    VR = [al.take(640, BF16) for i in range(2)]
    Pr = [al.take(T, BF16) for i in range(3)]
    wkd = al.take(KC * 256, BF16)
    osst = [al.take(4 * T, BF16) for i in range(2)]
    omst = [al.take(4 * T, BF16) for i in range(2)]
    tmp = [al.take(T, F32) for i in range(2)]
    r_QZ = [Res() for i in range(4)]
    r_QM, r_KD, r_V, r_wkd = Res(), [Res(), Res()], Res(), Res()
    r_Pr = [Res(), Res(), Res()]
    nit = [0]
    r_os, r_om, r_tmp = [Res(), Res()], [Res(), Res()], [Res(), Res()]
    OSv = g.OS.rearrange("c p n -> p c n")
    OMv = g.OM.rearrange("c p n -> p c n")
    for kv in range(2):
        for half in range(2):
            rec.op("pool", lambda e, kv=kv, half=half: e.dma_start(
                out=wkd.rearrange("p (c n) -> p c n", c=KC)[:, :, kv * 128 + half * 64: kv * 128 + half * 64 + 64],
                in_=wv[:, :, 3584 + kv * 64: 3584 + kv * 64 + 64]), writes=[r_wkd], dma="wkd")
    for cs in range(4):
        rec.op("pool", lambda e, cs=cs: e.memset(QZe[cs][64:128, :], 0.0), writes=[r_QZ[cs]])
        rec.op("pool", lambda e, cs=cs: e.memset(QZo[cs][0:64, :], 0.0), writes=[r_QZ[cs]])
    for kv in range(2):
        rec.op("pool", lambda e, kv=kv: e.memset(VL[kv], 0.0), writes=[r_V])
        rec.op("pool", lambda e, kv=kv: e.memset(VR[kv], 0.0), writes=[r_V])
    PS_O, PS_D, PS_OM, PS_DM = 2, 3, 4, 5
    npj = [0]

    def projbank():
        npj[0] += 1
        return 6 + npj[0] % 2

    nsw = [0]
    nme = [0]
    for t in range(NT):
        tok = t * T
        for cs in range(4):
            pi = projbank()
            ps = g.psum[pi]
            rec.op("pe", mm_group(ps, [(W(c, 3072 + cs * 128), U(c, tok, T)) for c in range(KC)]),
                   reads=[r_win, r_uT], writes=[g.psres[pi]])
            rec.op("dve", lambda e, cs=cs, ps=ps: e.tensor_scalar(out=QZe[cs][0:64, :], in0=ps[0:64, :], scalar1=0.125,
                                                                  scalar2=None, op0=ALU.mult),
                   reads=[g.psres[pi]], writes=[r_QZ[cs]])
            rec.op("act", lambda e, cs=cs, ps=ps: e.mul(out=QZo[cs][64:128, :], in_=ps[64:128, :], mul=0.125),
                   reads=[g.psres[pi]], writes=[r_QZ[cs]])
        for kv in range(2):
            pi = projbank()
            ps = g.psum[pi]
            rec.op("pe", mm_group(ps, [(wkd[:, c * 256 + kv * 128: c * 256 + (kv + 1) * 128], U(c, tok, T))
                                       for c in range(KC)]),
                   reads=[r_wkd, r_uT], writes=[g.psres[pi]])
            rec.op("dve", lambda e, kv=kv, ps=ps: e.tensor_copy(out=KD[kv][:, 128:640], in_=ps),
                   reads=[g.psres[pi]], writes=[r_KD[kv]])
        pi = projbank()
        ps = g.psum[pi]

        def vproj(e, ps=ps, tok=tok):
            ins = None
            for blk in range(4):
                for c in range(KC):
                    ins = e.matmul(ps[:, blk * 128:(blk + 1) * 128], lhsT=U(c, tok + blk * 128, 128),
                                   rhs=W(c, 3712), start=(c == 0), stop=(c == KC - 1))
            return ins
        rec.op("pe", vproj, reads=[r_win, r_uT], writes=[g.psres[pi]])
        ps3 = ps.rearrange("p (b n) -> p b n", b=4)
        for kv in range(2):
            vl3 = VL[kv].rearrange("p (b n) -> p b n", b=5)
            vr3 = VR[kv].rearrange("p (b n) -> p b n", b=5)
            rec.op("dve", lambda e, vl3=vl3, ps3=ps3, kv=kv: e.tensor_copy(out=vl3[:, 1:5, 0:64],
                                                                           in_=ps3[:, :, kv * 64:(kv + 1) * 64]),
                   reads=[g.psres[pi]], writes=[r_V])
            rec.op("act", lambda e, vr3=vr3, ps3=ps3, kv=kv: e.copy(out=vr3[:, 1:5, 64:128],
                                                                    in_=ps3[:, :, kv * 64:(kv + 1) * 64]),
                   reads=[g.psres[pi]], writes=[r_V])
        for hm in range(4):
            pi = projbank()
            ps = g.psum[pi]
            rec.op("pe", mm_group(ps, [(W(c, 3840 + hm * 128), U(c, tok, T)) for c in range(KC)]),
                   reads=[r_win, r_uT], writes=[g.psres[pi]])
            rec.op("act", lambda e, hm=hm, ps=ps: e.copy(out=QM[:, hm * T:(hm + 1) * T], in_=ps),
                   reads=[g.psres[pi]], writes=[r_QM])
        ob = t % 2
        items = [("s", cs, qb) for cs in range(4) for qb in range(4)] + [("m", hm, mc) for hm in range(4) for mc in range(2)]

        def emit_qk(it, i):
            sb_i = i % 2
            Sb = g.psum[sb_i]
            P = Pr[i % 3]
            r_P = r_Pr[i % 3]
            if it[0] == "s":
                _, cs, qb = it
                kv = cs // 2
                n = 4 * t + qb

                def qk(e, cs=cs, kv=kv, qb=qb, n=n, Sb=Sb):
                    ins = None
                    first = True
                    for hh, QZ in enumerate((QZe[cs], QZo[cs])):
                        for w in range(2):
                            if n == 0 and w == 0:
                                continue
                            ins = e.matmul(Sb[:, hh * 256 + w * 128: hh * 256 + (w + 1) * 128],
                                           lhsT=KD[kv][:, (qb + w) * 128:(qb + w + 1) * 128],
                                           rhs=QZ[:, qb * 128:(qb + 1) * 128], start=first, stop=False)
                            first = False
                    mb = C_SWAMB + 2 * cs * 256
                    if n == 0:
                        for hh in range(2):
                            ins = e.matmul(Sb[:, hh * 256 + 128: hh * 256 + 256], lhsT=g.identb,
                                           rhs=g.cb[:, mb + hh * 256 + 128: mb + hh * 256 + 256],
                                           start=False, stop=(hh == 1))
                    else:
                        ins = e.matmul(Sb[:, :], lhsT=g.identb, rhs=g.cb[:, mb: mb + 512], start=False, stop=True)
                    return ins
                rec.op("pe", qk, reads=[r_QZ[cs], r_KD[kv], g.r_const], writes=[g.psres[sb_i]])
                if n == 0:
                    rec.op("act", lambda e, P=P, Sb=Sb: e.activation(
                        out=P.rearrange("p (a b) -> p a b", a=2)[:, :, 128:256],
                        in_=Sb.rearrange("p (a b) -> p a b", a=2)[:, :, 128:256], func=AF.Exp),
                        reads=[g.psres[sb_i]], writes=[r_P])
                else:
                    rec.op("act", lambda e, P=P, Sb=Sb: e.activation(out=P, in_=Sb, func=AF.Exp),
                           reads=[g.psres[sb_i]], writes=[r_P])
            else:
                _, hm, mc = it
                rec.op("pe", mm_group(Sb, [(g.mkT[:, hm * MEM + mc * 128: hm * MEM + (mc + 1) * 128],
                                            QM[:, hm * T:(hm + 1) * T])]),
                       reads=[r_mk, r_QM], writes=[g.psres[sb_i]])
                rec.op("act", lambda e, P=P, Sb=Sb: e.activation(out=P, in_=Sb, func=AF.Exp, scale=128.0 ** -0.5),
                       reads=[g.psres[sb_i]], writes=[r_P])

        def emit_pv(it, i):
            P = Pr[i % 3]
            r_P = r_Pr[i % 3]
            grp = it[1] if it[0] == "s" else 4 + it[1]
            PS_O, PS_D = (2, 3) if grp % 2 == 0 else (4, 5)
            PS_OM, PS_DM = PS_O, PS_D
            if it[0] == "s":
                _, cs, qb = it
                kv = cs // 2
                n = 4 * t + qb

                def pv(e, P=P, kv=kv, qb=qb, n=n, PS_O=PS_O, PS_D=PS_D):
                    ins = None
                    for bank, Lm, Rm in ((PS_O, VL[kv], VR[kv]), (PS_D, None, None)):
                        items2 = []
                        for hh in range(2):
                            for w in range(2):
                                if n == 0 and w == 0:
                                    continue
                                items2.append((hh, w))
                        for ii, (hh, w) in enumerate(items2):
                            if Lm is None:
                                lhs = g.onesL if hh == 0 else g.onesR
                            else:
                                src = Lm if hh == 0 else Rm
                                lhs = src[:, (qb + w) * 128:(qb + w + 1) * 128]
                            ins = e.matmul(g.psum[bank][:, qb * 128:(qb + 1) * 128], lhsT=lhs,
                                           rhs=P[:, hh * 256 + w * 128: hh * 256 + (w + 1) * 128],
                                           start=(ii == 0), stop=(ii == len(items2) - 1))
                    return ins
                rec.op("pe", pv, reads=[r_P, r_V, g.r_const], writes=[g.psres[PS_O], g.psres[PS_D]])
                if qb == 3:
                    tb = cs % 2
                    rec.op("act", lambda e, tb=tb, cs=cs, PS_D=PS_D: e.activation(out=tmp[tb], in_=g.psum[PS_D], func=AF.Ln,
                                                                                 bias=sm[:, 2 + cs:3 + cs]),
                           reads=[g.psres[PS_D], r_sm], writes=[r_tmp[tb]])
                    rec.op("act", lambda e, tb=tb: e.activation(out=tmp[tb], in_=tmp[tb], func=AF.Exp, scale=-1.0),
                           reads=[r_tmp[tb]], writes=[r_tmp[tb]])
                    rec.op("dve", lambda e, tb=tb, cs=cs, ob=ob, PS_O=PS_O: e.tensor_tensor(out=osst[ob][:, cs * T:(cs + 1) * T],
                                                                          in0=g.psum[PS_O], in1=tmp[tb], op=ALU.mult),
                           reads=[g.psres[PS_O], r_tmp[tb]], writes=[r_os[ob]])
            else:
                _, hm, mc = it

                def pvm(e, P=P, hm=hm, mc=mc, PS_OM=PS_OM, PS_DM=PS_DM):
                    e.matmul(g.psum[PS_OM], lhsT=g.mv[:, mc * 512 + hm * 128: mc * 512 + (hm + 1) * 128], rhs=P,
                             start=(mc == 0), stop=(mc == 1))
                    return e.matmul(g.psum[PS_DM], lhsT=g.ones1, rhs=P, start=(mc == 0), stop=(mc == 1))
                rec.op("pe", pvm, reads=[r_P, r_mk, g.r_const], writes=[g.psres[PS_OM], g.psres[PS_DM]])
                if mc == 1:
                    tb = hm % 2
                    rec.op("act", lambda e, tb=tb, PS_DM=PS_DM: e.activation(out=tmp[tb], in_=g.psum[PS_DM], func=AF.Ln),
                           reads=[g.psres[PS_DM]], writes=[r_tmp[tb]])
                    rec.op("act", lambda e, tb=tb: e.activation(out=tmp[tb], in_=tmp[tb], func=AF.Exp, scale=-1.0),
                           reads=[r_tmp[tb]], writes=[r_tmp[tb]])
                    rec.op("dve", lambda e, tb=tb, hm=hm, ob=ob, PS_OM=PS_OM: e.tensor_tensor(out=omst[ob][:, hm * T:(hm + 1) * T],
                                                                          in0=g.psum[PS_OM], in1=tmp[tb], op=ALU.mult),
                           reads=[g.psres[PS_OM], r_tmp[tb]], writes=[r_om[ob]])

        for i, it in enumerate(items):
            emit_qk(it, nit[0] + i)
            if i >= 1:
                emit_pv(items[i - 1], nit[0] + i - 1)
        emit_pv(items[-1], nit[0] + len(items) - 1)
        nit[0] += len(items)
        rec.op("sp", lambda e, ob=ob, tok=tok: e.dma_start(out=OSv[:, :, tok:tok + T],
                                                           in_=osst[ob].rearrange("p (c n) -> p c n", c=4)),
               reads=[r_os[ob]], dma="osst%d" % ob)
        rec.op("sp", lambda e, ob=ob, tok=tok: e.dma_start(out=OMv[:, :, tok:tok + T],
                                                           in_=omst[ob].rearrange("p (c n) -> p c n", c=4)),
               reads=[r_om[ob]], dma="omst%d" % ob)
        if t + 1 < NT:
            for kv in range(2):
                rec.op("pool", lambda e, kv=kv: e.tensor_copy(out=KD[kv][:, 0:128], in_=KD[kv][:, 512:640]),
                       reads=[r_KD[kv]], writes=[r_KD[kv]])
                rec.op("pool", lambda e, kv=kv: e.tensor_copy(out=VL[kv][:, 0:128], in_=VL[kv][:, 512:640]),
                       reads=[r_V], writes=[r_V])
                rec.op("pool", lambda e, kv=kv: e.tensor_copy(out=VR[kv][:, 0:128], in_=VR[kv][:, 512:640]),
                       reads=[r_V], writes=[r_V])
    rec.barrier()
    if getattr(g, "stop_after", "") == "m1a":
        return

    al.off = xoff
    KA = [al.take(S, BF16) for m in range(2)]
    Vh = al.take(S, BF16)
    QA = [[al.take(T, BF16) for m in range(2)] for b in range(2)]
    Pd = [al.take(2 * T, BF16) for b in range(3)]
    rd = [[al.take(T, F32) for i in range(2)] for es in range(2)]
    tt = [[al.take(T, F32) for i in range(2)] for es in range(2)]
    pending = []
    gstep = [0]
    sqb = al.take(T, BF16)
    rstd2 = al.take(T, F32)
    ost = [al.take(T, BF16) for i in range(2)]
    r_KA, r_Vh = Res("KA"), Res("Vh")
    r_QA = [Res("QA0"), Res("QA1")]
    r_Pd = [Res(), Res(), Res()]
    r_rd, r_tt = [[Res(), Res()], [Res(), Res()]], [[Res(), Res()], [Res(), Res()]]
    r_sqb, r_rstd2, r_ost = Res(), Res(), [Res(), Res()]
    ODv = g.OD
    rec.op("pool", lambda e: e.memset(KA[0][64:128, :], 0.0), writes=[r_KA])
    rec.op("pool", lambda e: e.memset(KA[1][0:64, :], 0.0), writes=[r_KA])
    for b in range(2):
        rec.op("pool", lambda e, b=b: e.memset(QA[b][0][64:128, :], 0.0), writes=[r_QA[b]])
        rec.op("pool", lambda e, b=b: e.memset(QA[b][1][0:64, :], 0.0), writes=[r_QA[b]])
        rec.op("pool", lambda e, b=b: e.dma_start(out=QA[b][0][64:66, :], in_=g.qaug_d[:, :]),
               writes=[r_QA[b]], dma="qaug")
        rec.op("pool", lambda e, b=b: e.dma_start(out=QA[b][1][0:2, :], in_=g.qaug_d[:, :]),
               writes=[r_QA[b]], dma="qaug")
    O_B, D_B = (4, 5), (6, 7)
    Sslot = [g.PS[:, 0:1024], g.PS[:, 1024:2048]]
    r_Sslot = [[g.psres[0], g.psres[1]], [g.psres[2], g.psres[3]]]
    nep = [0]
    nbk = [0]
    nsl = [0]

    def bank4():
        nbk[0] += 1
        return nbk[0] % 4

    def qproj(h, qt):
        b = qt % 2
        pi = bank4()
        ps = g.psum[pi]
        rec.op("pe", mm_group(ps, [(W(c, h * 128), U(c, qt * T, T)) for c in range(KC)]),
               reads=[r_win, r_uT], writes=[g.psres[pi]])
        rec.op("dve", lambda e: e.tensor_scalar(out=QA[b][0][0:64, :], in0=ps[0:64, :], scalar1=0.125,
                                                scalar2=None, op0=ALU.mult),
               reads=[g.psres[pi]], writes=[r_QA[b]])
        rec.op("dve", lambda e: e.tensor_scalar(out=QA[b][1][64:128, :], in0=ps[64:128, :], scalar1=0.125,
                                                scalar2=None, op0=ALU.mult),
               reads=[g.psres[pi]], writes=[r_QA[b]])

    for h in range(8):
        slope = 2.0 ** (-(h + 1))
        rec.op("pool", lambda e, slope=slope: e.memset(KA[0][64:66, :], slope), writes=[r_KA])
        rec.op("pool", lambda e, slope=slope: e.memset(KA[1][0:2, :], slope), writes=[r_KA])
        for t8 in range(NT):
            pi = bank4()
            ps = g.psum[pi]
            rec.op("pe", mm_group(ps, [(W(c, 1024 + h * 128), U(c, t8 * T, T)) for c in range(KC)]),
                   reads=[r_win, r_uT], writes=[g.psres[pi]])
            rec.op("dve", lambda e, ps=ps, t8=t8: e.tensor_copy(out=KA[0][0:64, t8 * T:(t8 + 1) * T], in_=ps[0:64, :]),
                   reads=[g.psres[pi]], writes=[r_KA])
            rec.op("act", lambda e, ps=ps, t8=t8: e.copy(out=KA[1][64:128, t8 * T:(t8 + 1) * T], in_=ps[64:128, :]),
                   reads=[g.psres[pi]], writes=[r_KA])
        for g4 in range(8):
            pi = bank4()
            ps = g.psum[pi]

            def vproj2(e, ps=ps, g4=g4, h=h):
                ins = None
                for blk in range(4):
                    for c in range(KC):
                        ins = e.matmul(ps[:, blk * 128:(blk + 1) * 128], lhsT=U(c, (g4 * 4 + blk) * 128, 128),
                                       rhs=W(c, 2048 + h * 128), start=(c == 0), stop=(c == KC - 1))
                return ins
            rec.op("pe", vproj2, reads=[r_win, r_uT], writes=[g.psres[pi]])
            if g4 % 2 == 0:
                rec.op("dve", lambda e, ps=ps, g4=g4: e.tensor_copy(out=Vh[:, g4 * T:(g4 + 1) * T], in_=ps),
                       reads=[g.psres[pi]], writes=[r_Vh])
            else:
                rec.op("act", lambda e, ps=ps, g4=g4: e.copy(out=Vh[:, g4 * T:(g4 + 1) * T], in_=ps),
                       reads=[g.psres[pi]], writes=[r_Vh])
        qproj(h, 0)
        for qt in range(NT):
            b = qt % 2
            nkb = 4 * (qt + 1)

            def pv_ops(kb):
                pb = kb % 3
                dl = kb - 4 * qt
                c0 = 128 * dl if dl > 0 else 0

                def pv(e, pb=pb, c0=c0, kb=kb):
                    ins = None
                    for m in range(2):
                        e.matmul(g.psum[O_B[m]][:, c0:T], lhsT=Vh[:, kb * 128:(kb + 1) * 128],
                                 rhs=Pd[pb][:, m * T + c0:(m + 1) * T], start=(kb == 0), stop=(kb == nkb - 1))
                        ins = e.matmul(g.psum[D_B[m]][:, c0:T], lhsT=g.ones1, rhs=Pd[pb][:, m * T + c0:(m + 1) * T],
                                       start=(kb == 0), stop=(kb == nkb - 1))
                    return ins
                rec.op("pe", pv, reads=[r_Pd[pb], r_Vh, g.r_const],
                       writes=[g.psres[O_B[0]], g.psres[O_B[1]], g.psres[D_B[0]], g.psres[D_B[1]]])

            for kb in range(nkb):
                pb = kb % 3
                sl = nsl[0] % 2
                nsl[0] += 1
                dl = kb - 4 * qt
                c0 = 128 * dl if dl > 0 else 0

                def qk(e, kb=kb, dl=dl, c0=c0, b=b, sl=sl):
                    ins = None
                    for m in range(2):
                        Sb = Sslot[sl][:, m * T:(m + 1) * T]
                        ins = e.matmul(Sb[:, c0:T], lhsT=KA[m][:, kb * 128:(kb + 1) * 128], rhs=QA[b][m][:, c0:T],
                                       start=True, stop=(dl < 0))
                        if dl >= 0:
                            ins = e.matmul(Sb[:, c0:c0 + 128], lhsT=g.identb, rhs=g.tri, start=False, stop=True)
                    return ins
                rec.op("pe", qk, reads=[r_KA, r_QA[b], g.r_const], writes=r_Sslot[sl])
                bcol = h * 32 + dl + 28
                rec.op("act", lambda e, pb=pb, c0=c0, bcol=bcol, sl=sl: e.activation(
                    out=Pd[pb].rearrange("p (m n) -> p m n", m=2)[:, :, c0:T],
                    in_=Sslot[sl].rearrange("p (m n) -> p m n", m=2)[:, :, c0:T], func=AF.Exp,
                    bias=g.btab[:, bcol:bcol + 1]),
                    reads=r_Sslot[sl] + [g.r_const], writes=[r_Pd[pb]])
                if kb == 0 and qt + 1 < NT:
                    qproj(h, qt + 1)
                gstep[0] += 1
                if pending and gstep[0] >= pending[0][0]:
                    pending.pop(0)[1]()
                if kb >= 2:
                    pv_ops(kb - 2)
            pv_ops(nkb - 2)
            pv_ops(nkb - 1)
            es = nep[0] % 2
            ob = nep[0] % 2
            nep[0] += 1
            for m in range(2):
                rec.op("dve", lambda e, m=m, es=es: e.tensor_copy(out=rd[es][m], in_=g.psum[D_B[m]]),
                       reads=[g.psres[D_B[m]]], writes=[r_rd[es][m]])
            for m in range(2):
                rec.op("dve", lambda e, m=m, es=es: e.tensor_copy(out=tt[es][m], in_=g.psum[O_B[m]]),
                       reads=[g.psres[O_B[m]]], writes=[r_tt[es][m]])

            def part2(es=es, ob=ob, h=h, qt=qt):
                for m in range(2):
                    rec.op("dve", lambda e, m=m: e.reciprocal(out=rd[es][m], in_=rd[es][m]),
                           reads=[r_rd[es][m]], writes=[r_rd[es][m]])
                for m in range(2):
                    rec.op("dve", lambda e, m=m: e.tensor_tensor(out=tt[es][m], in0=tt[es][m], in1=rd[es][m], op=ALU.mult),
                           reads=[r_tt[es][m], r_rd[es][m]], writes=[r_tt[es][m]])
                rec.op("dve", lambda e: e.scalar_tensor_tensor(out=tt[es][0], in0=tt[es][1], scalar=sm[:, 0:1], in1=tt[es][0],
                                                               op0=ALU.mult, op1=ALU.add),
                       reads=[r_tt[es][0], r_tt[es][1], r_sm], writes=[r_tt[es][0]])
                rec.op("pool", lambda e: e.tensor_tensor(out=sqb, in0=tt[es][0], in1=tt[es][0], op=ALU.mult),
                       reads=[r_tt[es][0]], writes=[r_sqb])
                pi = bank4()
                ps = g.psum[pi]
                rec.op("pe", mm_group(ps, [(g.ones128, sqb)]), reads=[r_sqb, g.r_const], writes=[g.psres[pi]])
                rsqrt_ps(g, ps, g.psres[pi], rstd2, r_rstd2)
                rec.op("dve", lambda e: e.scalar_tensor_tensor(out=ost[ob], in0=tt[es][0], scalar=sm[:, 1:2], in1=rstd2,
                                                               op0=ALU.mult, op1=ALU.mult),
                       reads=[r_tt[es][0], r_rstd2, r_sm], writes=[r_ost[ob]])
                rec.op("sp", lambda e: e.dma_start(out=ODv[h, :, qt * T:(qt + 1) * T], in_=ost[ob]),
                       reads=[r_ost[ob]], dma="odst%d" % ob)
            pending.append((gstep[0] + 12, part2))
    while pending:
        pending.pop(0)[1]()
    rec.barrier()


def phase_m2(g, l):
    nc, rec = g.nc, g.rec
    al = Alloc(g.BIG)
    wg = al.take(KC * 3072, BF16)
    wbd = al.take(8 * D, BF16)
    wbs = al.take(4 * D, BF16)
    wbm = al.take(4 * D, BF16)
    wo = al.take(8 * D, BF16)
    woff = al.off
    stages = [al.take(4096, F32) for i in range(5)]
    al.off = woff
    xn = [al.take(KC * T, BF16) for i in range(2)]
    od = [al.take(8 * T, BF16) for i in range(2)]
    osm = [al.take(8 * T, BF16) for i in range(2)]
    rt = [al.take(KC * T, F32) for i in range(2)]
    mg = al.take(KC * T, BF16)
    sg = [al.take(T, F32) for i in range(3)]
    m1 = al.take(T, F32)
    m2 = al.take(T, F32)
    m3 = al.take(T, F32)
    r_w = Res("w")
    r_xn, r_od, r_osm, r_rt = [Res(), Res()], [Res(), Res()], [Res(), Res()], [Res(), Res()]
    r_mg, r_sg, r_m1, r_m2, r_m3 = Res(), [Res(), Res(), Res()], Res(), Res(), Res()
    Rv = g.R.rearrange("c p n -> p c n")
    XNv = g.XN.rearrange("c p n -> p c n")
    ODv = g.OD.rearrange("c p n -> p c n")
    OSv = g.OS.rearrange("c p n -> p c n")
    OMv = g.OM.rearrange("c p n -> p c n")
    wv = g.w_in[l].rearrange("(c p) n -> p c n", p=128)
    pieces = []
    for c in range(KC):
        pieces.append((wg[:, c * 3072:(c + 1) * 3072], wv[:, c, WIN_A:WIN_A + 3072]))
    for dst, src, nk in ((wbd, g.w_br_diff[l], 8), (wbs, g.w_br_swa[l], 4), (wbm, g.w_br_mem[l], 4), (wo, g.w_out[l], 8)):
        sv = src.rearrange("(k p) n -> p k n", p=128)
        for k0 in range(0, nk, 4):
            pieces.append((dst[:, k0 * D:(k0 + 4) * D], sv[:, k0:k0 + 4, :]))
    staged_load(g, pieces, stages, r_w)
    rec.barrier()

    def load(t):
        b = t % 2
        sl = slice(t * T, (t + 1) * T)
        rec.op("sp", lambda e: e.dma_start(out=xn[b].rearrange("p (c n) -> p c n", c=KC), in_=XNv[:, :, sl]),
               reads=[g.r_XN[t]], writes=[r_xn[b]], dma="m2xn%d" % b)
        rec.op("sp", lambda e: e.dma_start(out=od[b].rearrange("p (c n) -> p c n", c=8), in_=ODv[:, :, sl]),
               writes=[r_od[b]], dma="m2od%d" % b)
        rec.op("sp", lambda e: e.dma_start(out=osm[b][:, 0:4 * T].rearrange("p (c n) -> p c n", c=4), in_=OSv[:, :, sl]),
               writes=[r_osm[b]], dma="m2os%d" % b)
        rec.op("sp", lambda e: e.dma_start(out=osm[b][:, 4 * T:8 * T].rearrange("p (c n) -> p c n", c=4), in_=OMv[:, :, sl]),
               writes=[r_osm[b]], dma="m2os%d" % b)
        rec.op("sp", lambda e: e.dma_start(out=rt[b].rearrange("p (c n) -> p c n", c=KC), in_=Rv[:, :, sl]),
               reads=[g.r_R[t]], writes=[r_rt[b]], dma="m2rt%d" % b)

    load(0)
    for t in range(NT):
        b = t % 2
        if t + 1 < NT:
            load(t + 1)
        for oc in range(KC):
            for br in range(3):
                rec.op("pe", mm_group(g.psum[br], [(wg[:, c * 3072 + br * 1024 + oc * 128: c * 3072 + br * 1024 + (oc + 1) * 128],
                                                    xn[b][:, c * T:(c + 1) * T]) for c in range(KC)]),
                       reads=[r_w, r_xn[b]], writes=[g.psres[br]])
                rec.op("act", lambda e, br=br: e.activation(out=sg[br], in_=g.psum[br], func=AF.Sigmoid),
                       reads=[g.psres[br]], writes=[r_sg[br]])
            rec.op("pe", mm_group(g.psum[3], [(wbd[:, k * D + oc * 128: k * D + (oc + 1) * 128], od[b][:, k * T:(k + 1) * T])
                                              for k in range(8)]),
                   reads=[r_w, r_od[b]], writes=[g.psres[3]])
            rec.op("pe", mm_group(g.psum[4], [(wbs[:, k * D + oc * 128: k * D + (oc + 1) * 128], osm[b][:, k * T:(k + 1) * T])
                                              for k in range(4)]),
                   reads=[r_w, r_osm[b]], writes=[g.psres[4]])
            rec.op("pe", mm_group(g.psum[5], [(wbm[:, k * D + oc * 128: k * D + (oc + 1) * 128],
                                               osm[b][:, (4 + k) * T:(5 + k) * T]) for k in range(4)]),
                   reads=[r_w, r_osm[b]], writes=[g.psres[5]])
            rec.op("dve", lambda e: e.tensor_tensor(out=m1, in0=g.psum[3], in1=sg[0], op=ALU.mult),
                   reads=[g.psres[3], r_sg[0]], writes=[r_m1])
            rec.op("dve", lambda e: e.tensor_tensor(out=m2, in0=g.psum[4], in1=sg[1], op=ALU.mult),
                   reads=[g.psres[4], r_sg[1]], writes=[r_m2])
            rec.op("dve", lambda e: e.tensor_tensor(out=m3, in0=g.psum[5], in1=sg[2], op=ALU.mult),
                   reads=[g.psres[5], r_sg[2]], writes=[r_m3])
            rec.op("pool", lambda e: e.tensor_tensor(out=m1, in0=m1, in1=m2, op=ALU.add),
                   reads=[r_m1, r_m2], writes=[r_m1])
            rec.op("pool", lambda e, oc=oc: e.tensor_tensor(out=mg[:, oc * T:(oc + 1) * T], in0=m1, in1=m3, op=ALU.add),
                   reads=[r_m1, r_m3], writes=[r_mg])
        for oc in range(KC):
            pi = 6 + oc % 2
            rec.op("pe", mm_group(g.psum[pi], [(wo[:, k * D + oc * 128: k * D + (oc + 1) * 128], mg[:, k * T:(k + 1) * T])
                                               for k in range(8)]),
                   reads=[r_w, r_mg], writes=[g.psres[pi]])
            ro = rt[b][:, oc * T:(oc + 1) * T]
            rec.op("dve", lambda e, ro=ro, pi=pi: e.tensor_tensor(out=ro, in0=g.psum[pi], in1=ro, op=ALU.add),
                   reads=[g.psres[pi], r_rt[b]], writes=[r_rt[b]])
        rec.op("sp", lambda e, b=b, t=t: e.dma_start(out=Rv[:, :, t * T:(t + 1) * T],
                                                     in_=rt[b].rearrange("p (c n) -> p c n", c=KC)),
               reads=[r_rt[b]], writes=[g.r_R[t]], dma="m2st%d" % b)


def phase_final(g):
    nc, rec = g.nc, g.rec
    al = Alloc(g.BIG)
    rt = [al.take(KC * T, F32) for i in range(2)]
    sq = al.take(KC * T, BF16)
    rstd = al.take(T, F32)
    yt = al.take(KC * T, F32)
    ot = [al.take(4 * D, F32) for i in range(2)]
    r_rt, r_sq, r_rstd, r_yt, r_ot = [Res(), Res()], Res(), Res(), Res(), [Res(), Res()]
    Rv = g.R.rearrange("c p n -> p c n")
    gcol = VCOLS[("final_norm", 0)]
    npj = 0
    for t in range(NT):
        b = t % 2
        rec.op("sp", lambda e, b=b, t=t: e.dma_start(out=rt[b].rearrange("p (c n) -> p c n", c=KC),
                                                     in_=Rv[:, :, t * T:(t + 1) * T]),
               reads=[g.r_R[t]], writes=[r_rt[b]], dma="frt%d" % b)
        norm_tile(g, rt[b], r_rt[b], sq, r_sq, rstd, r_rstd, 7, yt, r_yt, gcol)
        for s4 in range(4):
            for half in range(2):
                pi = npj % 4
                npj += 1
                ps = g.psum[pi]

                def tr(e, ps=ps, s4=s4, half=half):
                    ins = None
                    for cc in range(4):
                        c = half * 4 + cc
                        ins = e.transpose(out=ps[:, cc * 128:(cc + 1) * 128],
                                          in_=yt[:, c * T + s4 * 128: c * T + (s4 + 1) * 128], identity=g.ident[:])
                    return ins
                rec.op("pe", tr, reads=[r_yt, g.r_const], writes=[g.psres[pi]])
                dst = ot[b][:, s4 * D + half * 512: s4 * D + (half + 1) * 512]
                if npj % 2 == 0:
                    rec.op("act", lambda e, dst=dst, ps=ps: e.copy(out=dst, in_=ps), reads=[g.psres[pi]], writes=[r_ot[b]])
                else:
                    rec.op("dve", lambda e, dst=dst, ps=ps: e.tensor_copy(out=dst, in_=ps), reads=[g.psres[pi]], writes=[r_ot[b]])
        dsto = g.out[t * T:(t + 1) * T, :].rearrange("(s p) d -> p s d", p=128)
        rec.op("sp", lambda e, b=b, dsto=dsto: e.dma_start(out=dsto, in_=ot[b].rearrange("p (s d) -> p s d", s=4)),
               reads=[r_ot[b]], dma="fout%d" % b)


def make_vecs(inp):
    v = np.zeros((128, NV), np.float32)
    for l in range(DEPTH):
        for name in ("ffn1_norm", "mix_norm", "ffn2_norm", "mem_norm"):
            c = VCOLS[(name, l)]
            v[:, c:c + KC] = np.asarray(inp[name][l]).reshape(KC, 128).T
        v[:, VCOLS[("diff_subnorm", l)]] = np.asarray(inp["diff_subnorm"][l])
    c = VCOLS[("final_norm", 0)]
    v[:, c:c + KC] = np.asarray(inp["final_norm"]).reshape(KC, 128).T
    return v


def make_consts():
    c = np.zeros((128, NC16), np.float32)
    j = np.arange(128)[:, None].astype(np.float64)
    i = np.arange(128)[None, :].astype(np.float64)
    c[:, C_IDENT:C_IDENT + 128] = np.eye(128)
    c[:, C_TRI:C_TRI + 128] = np.where(j > i, NEG, 0.0)
    for h in range(8):
        sl = 2.0 ** (-(h + 1))
        prev = np.where(j > i, -sl * (i + 128 - j), NEG)
        cur = np.where(j <= i, -sl * (i - j), NEG)
        c[:, C_SWAMB + h * 256: C_SWAMB + h * 256 + 128] = prev
        c[:, C_SWAMB + h * 256 + 128: C_SWAMB + (h + 1) * 256] = cur
    c[:, C_ONESL:C_ONESL + 64] = 1.0
    c[:, C_ONESR + 64:C_ONESR + 128] = 1.0
    c[:, C_ONES128:C_ONES128 + 128] = 1.0 / 128
    c[:, C_ONES1:C_ONES1 + 128] = 1.0
    il = np.arange(T)
    qaug = np.stack([-(il - il % 2), -(il % 2)]).astype(np.float32)
    bt = np.zeros((128, 256), np.float32)
    for h in range(8):
        sl = 2.0 ** (-(h + 1))
        for di in range(32):
            bt[:, h * 32 + di] = sl * (np.arange(128) + 128 * (di - 28))
    return c, qaug, bt


def make_in_maps(inp):
    f = lambda a: np.ascontiguousarray(np.asarray(a, dtype=np.float32))
    shared = {
        "ffn1_wi": f(inp["ffn1_wi"]), "ffn2_wi": f(inp["ffn2_wi"]),
        "ffn1_wo": f(inp["ffn1_wo"]), "ffn2_wo": f(inp["ffn2_wo"]),
        "w_in": f(inp["w_in"]), "w_mem_kv": f(inp["w_mem_kv"]),
        "w_br_diff": f(inp["w_br_diff"]), "w_br_swa": f(inp["w_br_swa"]),
        "w_br_mem": f(inp["w_br_mem"]), "w_out": f(inp["w_out"]),
        "vecs": make_vecs(inp),
        "ident": np.eye(128, dtype=np.float32),
        "diff_lambda": f(inp["diff_lambda"]).reshape(DEPTH, 256),
        "swa_sinks": f(inp["swa_sinks"]),
    }
    shared["cf32"], shared["qaug"], shared["btab"] = make_consts()
    maps = []
    for b in range(NB):
        m = dict(shared)
        m["x"] = f(inp["x"][b])
        m["mem"] = f(inp["mem"][b])
        maps.append(m)
    return maps


def kernel(**inp):
    nc = build()
    res = run_bass_kernel_spmd(nc, make_in_maps(inp), core_ids=list(range(NB)))
    return np.stack([np.asarray(r["out"]) for r in res.results], axis=0).astype(np.float32)
```

```python
import numpy as np
from contextlib import ExitStack
import concourse.bass as bass
import concourse.mybir as mybir
from concourse.bass_utils import run_bass_kernel_spmd

F32 = mybir.dt.float32
BF16 = mybir.dt.bfloat16
ALU = mybir.AluOpType
AF = mybir.ActivationFunctionType

D = 1024
S = 4096
NB = 8
DEPTH = 2
MEM = 256
DFF = 2816
INW = 7424
EPS = 1e-6
T = 512
BIGB = 200 * 1024
NT = S // T
KC = D // 128
C_IDENT, C_TRI, C_SWAMB, C_ONESL, C_ONESR, C_ONES128, C_ONES1 = 0, 128, 256, 2304, 2432, 2560, 2688
NC16 = 2816
NEG = -30000.0
WIN_A = 4352


class Res:
    __slots__ = ("name", "writer", "readers")

    def __init__(self, name=""):
        self.name = name
        self.writer = None
        self.readers = []


class Op:
    __slots__ = ("eng", "fn", "deps", "dma", "tok", "needed", "idx")

    def __init__(self, eng, fn, dma):
        self.eng = eng
        self.fn = fn
        self.dma = dma
        self.deps = []
        self.tok = None
        self.needed = False
        self.idx = None


ENGS = ("pe", "act", "dve", "pool", "sp")


class Rec:
    def __init__(self, nc, stack):
        self.nc = nc
        self.stack = stack
        self.ops = {e: [] for e in ENGS}
        self.esem = {e: stack.enter_context(nc.semaphore("s_" + e)) for e in ENGS}
        self.dsem = {}
        self.nops = 0

    def _dma_sem(self, key):
        if key not in self.dsem:
            h = self.stack.enter_context(self.nc.semaphore("d_" + key))
            self.dsem[key] = [h, 0]
        return self.dsem[key]

    def op(self, eng, fn, reads=(), writes=(), dma=None):
        o = Op(eng, fn, dma)
        deps = {}
        for r in reads:
            w = r.writer
            if w is not None:
                deps[id(w)] = (w, True)
        for r in writes:
            w = r.writer
            if w is not None and id(w) not in deps:
                deps[id(w)] = (w, True)
            for rd in r.readers:
                if id(rd) not in deps:
                    deps[id(rd)] = (rd, False)
        for d, strong in deps.values():
            if d is o:
                continue
            if d.dma is None and dma is None and d.eng == eng:
                if eng == "pe" or not strong:
                    continue
            o.deps.append(d)
            d.needed = True
        for r in reads:
            r.readers.append(o)
        for r in writes:
            r.writer = o
            r.readers = []
        if dma is not None:
            s = self._dma_sem(dma)
            s[1] += 16
            o.tok = (s[0], s[1])
        self.ops[eng].append(o)
        self.nops += 1
        return o

    def barrier(self, exclude=()):
        last = []
        for e in ENGS:
            for o in reversed(self.ops[e]):
                if o.dma is None and o.fn is not None:
                    last.append(o)
                    break
        dtoks = [(s[0], s[1]) for k, s in self.dsem.items() if s[1] > 0 and k not in exclude]
        for e in ENGS:
            o = Op(e, None, None)
            for d in last:
                if d.eng != e:
                    o.deps.append(d)
                    d.needed = True
            o.tok = None
            o.idx = dtoks
            self.ops[e].append(o)

    def emit(self):
        nc = self.nc
        for e in ENGS:
            n = 0
            for o in self.ops[e]:
                if o.dma is None and o.needed:
                    n += 1
                    o.tok = (self.esem[e], n)
        with nc.Block() as block:
            def run(ename):
                def body(eng):
                    waited = {}
                    for o in self.ops[ename]:
                        want = {}
                        for d in o.deps:
                            sem, val = d.tok
                            k = id(sem)
                            if k not in want or want[k][1] < val:
                                want[k] = (sem, val)
                        if o.fn is None and isinstance(o.idx, list):
                            for sem, val in o.idx:
                                k = id(sem)
                                if k not in want or want[k][1] < val:
                                    want[k] = (sem, val)
                        for k, (sem, val) in want.items():
                            if waited.get(k, 0) >= val:
                                continue
                            eng.wait_ge(sem, val)
                            waited[k] = val
                        if o.fn is None:
                            continue
                        ins = o.fn(eng)
                        if o.dma is not None:
                            ins.then_inc(o.tok[0], 16)
                        elif o.needed:
                            ins.then_inc(o.tok[0], 1)
                return body
            block.tensor(run("pe"))
            block.scalar(run("act"))
            block.vector(run("dve"))
            block.gpsimd(run("pool"))
            block.sync(run("sp"))


def vec_cols():
    cols = {}
    n = 0
    for l in range(DEPTH):
        for name in ("ffn1_norm", "mix_norm", "ffn2_norm", "mem_norm"):
            cols[(name, l)] = n
            n += KC
        cols[("diff_subnorm", l)] = n
        n += 1
    cols[("final_norm", 0)] = n
    n += KC
    return cols, n


VCOLS, NV = vec_cols()


class Ctx:
    pass


class Alloc:
    def __init__(self, big):
        self.big = big
        self.off = 0

    def take(self, n, dt):
        nbytes = n * (4 if dt == F32 else 2)
        assert self.off % 4 == 0
        a = self.big[:, self.off // 2:(self.off + nbytes) // 2]
        self.off += nbytes
        assert self.off <= BIGB, self.off
        return a.bitcast(F32) if dt == F32 else a


def build(n_phases=100, debug_out=None, stop_after=""):
    nc = bass.Bass("TRN2", target_bir_lowering=False)
    stack = ExitStack()
    with stack:
        _build(nc, stack, n_phases, debug_out, stop_after)
    return nc


def _build(nc, stack, n_phases, debug_out, stop_after=""):
    rec = Rec(nc, stack)
    g = Ctx()
    g.stop_after = stop_after
    g.nc, g.rec = nc, rec

    def din(name, shape, dt=F32):
        return nc.dram_tensor(name, list(shape), dt, kind="ExternalInput").ap()

    g.x = din("x", [S, D])
    g.mem = din("mem", [MEM, D])
    g.ffn_wi = [din("ffn1_wi", [DEPTH, D, 2 * DFF]), din("ffn2_wi", [DEPTH, D, 2 * DFF])]
    g.ffn_wo = [din("ffn1_wo", [DEPTH, DFF, D]), din("ffn2_wo", [DEPTH, DFF, D])]
    g.w_in = din("w_in", [DEPTH, D, INW])
    g.w_mem_kv = din("w_mem_kv", [DEPTH, D, D])
    g.w_br_diff = din("w_br_diff", [DEPTH, D, D])
    g.w_br_swa = din("w_br_swa", [DEPTH, 512, D])
    g.w_br_mem = din("w_br_mem", [DEPTH, 512, D])
    g.w_out = din("w_out", [DEPTH, D, D])
    g.vecs_d = din("vecs", [128, NV])
    g.ident_d = din("ident", [128, 128])
    g.cf32_d = din("cf32", [128, NC16])
    g.qaug_d = din("qaug", [2, T])
    g.btab_d = din("btab", [128, 256])
    g.dlam_d = din("diff_lambda", [DEPTH, 256])
    g.sinks_d = din("swa_sinks", [DEPTH, 8])
    g.out = nc.dram_tensor("out", [S, D], F32, kind="ExternalOutput").ap()

    def scratch(name, shape, dt):
        kind = "ExternalOutput" if (debug_out and name in debug_out) else "Internal"
        return nc.dram_tensor(name, list(shape), dt, kind=kind).ap()

    g.R = scratch("R", [KC, 128, S], F32)
    g.XN = scratch("XN", [KC, 128, S], BF16)
    g.OD = scratch("OD", [8, 128, S], BF16)
    g.OS = scratch("OS", [4, 128, S], BF16)
    g.OM = scratch("OM", [4, 128, S], BF16)

    def sb(name, shape, dt):
        return stack.enter_context(nc.sbuf_tensor(name, list(shape), dt))

    g.vecs = sb("vecs_sb", [128, NV], F32)
    g.ident = sb("ident_sb", [128, 128], F32)
    g.onesD = sb("onesD", [128, 128], BF16)
    g.BIG = sb("BIG", [128, BIGB // 2], BF16)
    g.r_R = [Res("R%d" % t) for t in range(NT)]
    g.r_XN = [Res("XN%d" % t) for t in range(NT)]
    g.PS = stack.enter_context(nc.psum_tensor("PS", [128, 4096], F32))
    g.psum = [g.PS[:, i * 512:(i + 1) * 512] for i in range(8)]
    g.psres = [Res("ps%d" % i) for i in range(8)]

    r_const = Res("const")
    rec.op("sp", lambda e: e.dma_start(out=g.vecs[:], in_=g.vecs_d[:, :]), writes=[r_const], dma="const")
    rec.op("sp", lambda e: e.dma_start(out=g.ident[:], in_=g.ident_d[:, :]), writes=[r_const], dma="const")
    rec.op("dve", lambda e: e.memset(g.onesD[:], 1.0 / D), writes=[r_const])
    g.epsc = sb("epsc", [128, 1], F32)
    rec.op("dve", lambda e: e.memset(g.epsc[:], EPS), writes=[r_const])
    g.cb = sb("cb", [128, NC16], BF16)
    for i in range(0, NC16, 1408):
        rec.op("pool", lambda e, i=i: e.dma_start(out=g.cb[:, i:i + 1408], in_=g.cf32_d[:, i:i + 1408]),
               writes=[r_const], dma="const2")
    g.btab = sb("btab_sb", [128, 256], F32)
    rec.op("sp", lambda e: e.dma_start(out=g.btab[:], in_=g.btab_d[:, :]), writes=[r_const], dma="const")
    g.identb = g.cb[:, C_IDENT:C_IDENT + 128]
    g.tri = g.cb[:, C_TRI:C_TRI + 128]
    g.onesL = g.cb[:, C_ONESL:C_ONESL + 128]
    g.onesR = g.cb[:, C_ONESR:C_ONESR + 128]
    g.ones128 = g.cb[:, C_ONES128:C_ONES128 + 128]
    g.ones1 = g.cb[:, C_ONES1:C_ONES1 + 128]
    g.small = sb("small", [128, 32], F32)
    g.r_const = r_const
    rec.barrier()

    phases = []
    phases.append(lambda: phase_t0(g))
    for l in range(DEPTH):
        phases.append(lambda l=l: phase_ffn(g, l, 0))
        phases.append(lambda l=l: phase_m1(g, l))
        phases.append(lambda l=l: phase_m2(g, l))
        phases.append(lambda l=l: phase_ffn(g, l, 1))
    phases.append(lambda: phase_final(g))
    g.pref = {}
    g.bar_excl = ()
    for i, p in enumerate(phases):
        if i >= n_phases:
            break
        p()
        rec.barrier(exclude=g.bar_excl)
    rec.emit()


def phase_t0(g):
    nc, rec = g.nc, g.rec
    al = Alloc(g.BIG)
    xin = [al.take(4096, F32) for i in range(2)]
    rt = [al.take(4096, F32) for i in range(2)]
    r_xin = [Res("xin0"), Res("xin1")]
    r_rt = [Res("rt0"), Res("rt1")]
    Rv = g.R.rearrange("c p n -> p c n")
    for t in range(NT):
        b = t % 2
        src = g.x[t * T:(t + 1) * T, :].rearrange("(s p) d -> p s d", p=128)
        dst = xin[b].rearrange("p (s d) -> p s d", s=4)
        rec.op("sp", lambda e, dst=dst, src=src: e.dma_start(out=dst, in_=src),
               writes=[r_xin[b]], dma="xin%d" % b)
        for c in range(KC):
            pb = c % 4
            ps = g.psum[pb]

            def tr(e, b=b, c=c, ps=ps):
                ins = None
                for s in range(4):
                    ins = e.transpose(out=ps[:, s * 128:(s + 1) * 128],
                                      in_=xin[b][:, s * 1024 + c * 128: s * 1024 + (c + 1) * 128],
                                      identity=g.ident[:])
                return ins
            rec.op("pe", tr, reads=[r_xin[b], g.r_const], writes=[g.psres[pb]])
            dsl = rt[b][:, c * 512:(c + 1) * 512]
            if c % 2 == 0:
                rec.op("act", lambda e, dsl=dsl, ps=ps: e.copy(out=dsl, in_=ps[:]),
                       reads=[g.psres[pb]], writes=[r_rt[b]])
            else:
                rec.op("dve", lambda e, dsl=dsl, ps=ps: e.tensor_copy(out=dsl, in_=ps[:]),
                       reads=[g.psres[pb]], writes=[r_rt[b]])
        src = rt[b].rearrange("p (c n) -> p c n", c=KC)
        rec.op("sp", lambda e, src=src, t=t: e.dma_start(out=Rv[:, :, t * T:(t + 1) * T], in_=src),
               reads=[r_rt[b]], writes=[g.r_R[t]], dma="rtst%d" % b)


def rsqrt_ps(g, ps, r_ps, rstd, r_rstd):
    rec = g.rec
    rec.op("act", lambda e: e.activation(out=rstd, in_=ps[:], func=AF.Ln, bias=g.epsc[:, 0:1]),
           reads=[r_ps, g.r_const], writes=[r_rstd])
    rec.op("act", lambda e: e.activation(out=rstd, in_=rstd, func=AF.Exp, scale=-0.5),
           reads=[r_rstd], writes=[r_rstd])


def staged_load(g, pieces, stages, r_dst_default=None):
    rec = g.rec
    ns = len(stages)
    r_st = [Res() for _ in range(ns)]
    for i, pc in enumerate(pieces):
        dst, src = pc[0], pc[1]
        r_dst = pc[2] if len(pc) > 2 else r_dst_default
        k = i % ns
        n = 1
        for d in src.shape[1:]:
            n *= d
        assert n <= 4096
        st = stages[k][:, 0:n]
        st_in = st.rearrange("p (a b) -> p a b", a=src.shape[1]) if len(src.shape) == 3 else st
        rec.op("sp", lambda e, st_in=st_in, src=src: e.dma_start(out=st_in, in_=src), writes=[r_st[k]],
               dma="stg%d" % k)
        eng = ("dve", "pool", "act")[i % 3]
        if eng == "act":
            rec.op(eng, lambda e, dst=dst, st=st: e.copy(out=dst, in_=st), reads=[r_st[k]], writes=[r_dst])
        else:
            rec.op(eng, lambda e, dst=dst, st=st: e.tensor_copy(out=dst, in_=st), reads=[r_st[k]], writes=[r_dst])


def norm_tile(g, rt, r_rt, sq, r_sq, rstd, r_rstd, ps_i, xn, r_xn, gcol, xn_stride=T, part=None):
    rec = g.rec
    if part in (None, "a"):
        rec.op("act", lambda e: e.activation(out=sq, in_=rt, func=AF.Square),
               reads=[r_rt], writes=[r_sq])
    if part == "a":
        return
    ps = g.psum[ps_i]

    def mm(e):
        ins = None
        for c in range(KC):
            ins = e.matmul(ps[:], lhsT=g.onesD[:], rhs=sq[:, c * T:(c + 1) * T],
                           start=(c == 0), stop=(c == KC - 1))
        return ins
    rec.op("pe", mm, reads=[r_sq, g.r_const], writes=[g.psres[ps_i]])
    rsqrt_ps(g, ps, g.psres[ps_i], rstd, r_rstd)
    for c in range(KC):
        rec.op("dve", lambda e, c=c: e.scalar_tensor_tensor(
            out=xn[:, c * xn_stride:c * xn_stride + T], in0=rt[:, c * T:(c + 1) * T],
            scalar=g.vecs[:, gcol + c:gcol + c + 1], in1=rstd, op0=ALU.mult, op1=ALU.mult),
            reads=[r_rt, r_rstd, g.r_const], writes=[r_xn])


def phase_ffn(g, l, which):
    nc, rec = g.nc, g.rec
    al = Alloc(g.BIG)
    HJ = 11
    HW = HJ * 128
    wi_sb, wo_sb = [None, None], [None, None]
    wi_sb[0] = al.take(KC * 2 * HW, BF16)
    wo_sb[0] = al.take(HJ * D, BF16)
    al.off = 69632
    wi_sb[1] = al.take(KC * 2 * HW, BF16)
    wo_sb[1] = al.take(HJ * D, BF16)
    woff = al.off
    stages = [al.take(4096, F32) for i in range(3)]
    al.off = woff
    rt = [al.take(KC * T, F32) for i in range(2)]
    rstd = al.take(T, F32)
    sa = [al.take(T, F32) for i in range(2)]
    xn = al.take(KC * T, BF16)
    hid = al.take(HJ * T, BF16)
    sq = al.take(KC * T, BF16)
    r_w = [Res("w0"), Res("w1")]
    r_rt = [Res("rt0"), Res("rt1")]
    r_rstd, r_xn, r_hid, r_sq = Res("rstd"), Res("xn"), Res("hid"), Res("sq")
    r_sa = [Res("sa0"), Res("sa1")]
    wi_d = g.ffn_wi[which][l].rearrange("(c p) n -> p c n", p=128)
    wo_d = g.ffn_wo[which][l]
    gcol = VCOLS[("ffn1_norm" if which == 0 else "ffn2_norm", l)]
    Rv = g.R.rearrange("c p n -> p c n")
    XNv = g.XN.rearrange("c p n -> p c n")

    for h in (1,):
        wv = wi_sb[h].rearrange("p (c n) -> p c n", c=KC)
        for part in range(2):
            for c in range(KC):
                src = wi_d[:, c, part * DFF + h * HW: part * DFF + (h + 1) * HW]
                dst = wv[:, c, part * HW:(part + 1) * HW]
                rec.op("pool", lambda e, dst=dst, src=src: e.dma_start(out=dst, in_=src),
                       writes=[r_w[h]], dma="w%d" % h)
        src = wo_d[h * HW:(h + 1) * HW, :].rearrange("(j p) n -> p j n", p=128)
        dst = wo_sb[h].rearrange("p (j n) -> p j n", j=HJ)
        rec.op("pool", lambda e, dst=dst, src=src: e.dma_start(out=dst, in_=src),
               writes=[r_w[h]], dma="w%d" % h)
    pieces = []
    for c in range(KC):
        src = wi_d[:, c, :].rearrange("p (two n) -> p two n", two=2)[:, :, 0:HW]
        pieces.append((wi_sb[0][:, c * 2 * HW:(c + 1) * 2 * HW], src))
    wo3 = wo_d[0:HW, :].rearrange("(j p) n -> p j n", p=128)
    for j0 in range(0, HJ, 4):
        j1 = min(HJ, j0 + 4)
        pieces.append((wo_sb[0][:, j0 * D:j1 * D], wo3[:, j0:j1, :]))
    pk = ("ffn0", l, which)
    if pk in g.pref:
        r_w[0] = g.pref.pop(pk)
    else:
        staged_load(g, pieces, stages, r_w[0])
        rec.barrier(exclude=("w1", "pf"))
    g.bar_excl = ()

    def prefetch_next():
        if which == 0:
            r_new = Res("pf_win")
            wvn = g.w_in[l].rearrange("(c p) n -> p c n", p=128)
            for c in range(KC):
                for i in range(0, WIN_A, 1088):
                    rec.op("pool", lambda e, c=c, i=i: e.dma_start(
                        out=g.BIG[:, c * WIN_A + i: c * WIN_A + i + 1088], in_=wvn[:, c, i:i + 1088]),
                        writes=[r_w[0], r_new], dma="pf")
            g.pref[("win", l)] = r_new
            g.bar_excl = ("pf",)
        elif l + 1 < DEPTH:
            r_new = Res("pf_ffn")
            wi_n = g.ffn_wi[0][l + 1].rearrange("(c p) n -> p c n", p=128)
            wo_n = g.ffn_wo[0][l + 1]
            wvv = wi_sb[0].rearrange("p (c n) -> p c n", c=KC)
            for part in range(2):
                for c in range(KC):
                    src = wi_n[:, c, part * DFF: part * DFF + HW]
                    dst = wvv[:, c, part * HW:(part + 1) * HW]
                    rec.op("pool", lambda e, dst=dst, src=src: e.dma_start(out=dst, in_=src),
                           writes=[r_w[0], r_new], dma="pf")
            src = wo_n[0:HW, :].rearrange("(j p) n -> p j n", p=128)
            dst = wo_sb[0].rearrange("p (j n) -> p j n", j=HJ)
            rec.op("pool", lambda e, dst=dst, src=src: e.dma_start(out=dst, in_=src),
                   writes=[r_w[0], r_new], dma="pf")
            g.pref[("ffn0", l + 1, 0)] = r_new
            g.bar_excl = ("pf",)

    def load_b(t, full=False):
        b = t % 2
        if full:
            norm_tile(g, rt[b], r_rt[b], sq, r_sq, rstd, r_rstd, 6, xn, r_xn, gcol, part="b")
        src = xn.rearrange("p (c n) -> p c n", c=KC)
        rec.op("sp", lambda e: e.dma_start(out=XNv[:, :, t * T:(t + 1) * T], in_=src),
               reads=[r_xn], writes=[g.r_XN[t]], dma="xnst")

    def load(h, t, split=False):
        b = t % 2
        dst = rt[b].rearrange("p (c n) -> p c n", c=KC)
        rec.op("sp", lambda e: e.dma_start(out=dst, in_=Rv[:, :, t * T:(t + 1) * T]),
               reads=[g.r_R[t]], writes=[r_rt[b]], dma="rtld%d" % b)
        if h == 0:
            norm_tile(g, rt[b], r_rt[b], sq, r_sq, rstd, r_rstd, 6, xn, r_xn, gcol, part=("a" if split else None))
            if not split:
                load_b(t)
        else:
            dstx = xn.rearrange("p (c n) -> p c n", c=KC)
            rec.op("sp", lambda e: e.dma_start(out=dstx, in_=XNv[:, :, t * T:(t + 1) * T]),
                   reads=[g.r_XN[t]], writes=[r_xn], dma="xnld")

    for h in range(2):
        if h == 1:
            prefetch_next()
        load(h, 0)
        for t in range(NT):
            b = t % 2
            for j in range(HJ):
                for part in range(2):
                    pi = part * 2 + j % 2
                    ps = g.psum[pi]

                    def mm(e, ps=ps, part=part, j=j, h=h):
                        ins = None
                        for c in range(KC):
                            o0 = c * 2 * HW + part * HW + j * 128
                            ins = e.matmul(ps[:], lhsT=wi_sb[h][:, o0:o0 + 128], rhs=xn[:, c * T:(c + 1) * T],
                                           start=(c == 0), stop=(c == KC - 1))
                        return ins
                    rec.op("pe", mm, reads=[r_w[h], r_xn], writes=[g.psres[pi]])
                psa, psb = g.psum[j % 2], g.psum[2 + j % 2]
                sj = sa[j % 2]
                rec.op("act", lambda e, psa=psa, sj=sj: e.activation(out=sj, in_=psa[:], func=AF.Silu),
                       reads=[g.psres[j % 2]], writes=[r_sa[j % 2]])
                hj = hid[:, j * T:(j + 1) * T]
                rec.op("dve", lambda e, hj=hj, sj=sj, psb=psb: e.tensor_tensor(out=hj, in0=sj, in1=psb[:], op=ALU.mult),
                       reads=[r_sa[j % 2], g.psres[2 + j % 2]], writes=[r_hid])
            if t + 1 < NT:
                load(h, t + 1, split=True)
            for oc in range(KC):
                if oc == 4 and h == 0 and t + 1 < NT:
                    load_b(t + 1, full=True)
                pi = 4 + oc % 2
                ps = g.psum[pi]

                def mm2(e, ps=ps, oc=oc, h=h):
                    ins = None
                    for j in range(HJ):
                        ins = e.matmul(ps[:], lhsT=wo_sb[h][:, j * D + oc * 128: j * D + (oc + 1) * 128],
                                       rhs=hid[:, j * T:(j + 1) * T], start=(j == 0), stop=(j == HJ - 1))
                    return ins
                rec.op("pe", mm2, reads=[r_w[h], r_hid], writes=[g.psres[pi]])
                ro = rt[b][:, oc * T:(oc + 1) * T]
                rec.op("dve", lambda e, ro=ro, ps=ps: e.scalar_tensor_tensor(
                    out=ro, in0=ps[:], scalar=0.5, in1=ro, op0=ALU.mult, op1=ALU.add),
                    reads=[g.psres[pi], r_rt[b]], writes=[r_rt[b]])
            src = rt[b].rearrange("p (c n) -> p c n", c=KC)
            rec.op("sp", lambda e, src=src, t=t: e.dma_start(out=Rv[:, :, t * T:(t + 1) * T], in_=src),
                   reads=[r_rt[b]], writes=[g.r_R[t]], dma="rtst%d" % b)


def mm_group(ps, pairs):
    def fn(e):
        ins = None
        n = len(pairs)
        for i, (l, r) in enumerate(pairs):
            ins = e.matmul(ps, lhsT=l, rhs=r, start=(i == 0), stop=(i == n - 1))
        return ins
    return fn


def phase_m1(g, l):
    import math
    nc, rec = g.nc, g.rec
    al = Alloc(g.BIG)
    win_sb = al.take(KC * WIN_A, BF16)
    uT = al.take(KC * S, BF16)
    g.mkT = al.take(4 * MEM, BF16)
    g.mv = al.take(2 * 512, BF16)
    xoff = al.off
    r_win, r_uT, r_sm = Res("win"), Res("uT"), Res("sm")
    sm = g.small
    lp = uT[:, 0:1024].bitcast(F32)
    stages = [al.take(4096, F32) for i in range(3)]
    wmem_sb = al.take(KC * D, BF16)
    al.off = xoff
    lam_init = 0.8 - 0.6 * math.exp(-0.3 * l)
    Rv = g.R.rearrange("c p n -> p c n")
    XNv = g.XN.rearrange("c p n -> p c n")

    def W(c, col, n=128):
        return win_sb[:, c * WIN_A + col: c * WIN_A + col + n]

    def U(c, tok, n):
        return uT[:, c * S + tok: c * S + tok + n]

    wv = g.w_in[l].rearrange("(c p) n -> p c n", p=128)
    wmv = g.w_mem_kv[l].rearrange("(c p) n -> p c n", p=128)
    r_wmem = Res("wmem")
    pieces = []
    if ("win", l) in g.pref:
        r_win = g.pref.pop(("win", l))
    else:
        for c in range(KC):
            for i in (0, 2176):
                pieces.append((W(c, i, 2176), wv[:, c, i:i + 2176]))
    g.bar_excl = ()
    pieces.append((wmem_sb[:, 0:4096], wmv[:, 0:4, :], r_wmem))
    pieces.append((wmem_sb[:, 4096:8192], wmv[:, 4:8, :], r_wmem))
    staged_load(g, pieces, stages, r_win)

    rec.op("sp", lambda e: e.dma_start(out=lp[:, 256:512], in_=g.dlam_d[l].partition_broadcast(128)),
           writes=[r_sm], dma="sm")
    rec.op("sp", lambda e: e.dma_start(out=sm[:, 8:16], in_=g.sinks_d[l].partition_broadcast(128)),
           writes=[r_sm], dma="sm")
    rec.op("dve", lambda e: e.tensor_tensor(out=lp[:, 32:96], in0=lp[:, 256:320], in1=lp[:, 320:384], op=ALU.mult),
           reads=[r_sm], writes=[r_sm])
    rec.op("dve", lambda e: e.tensor_tensor(out=lp[:, 96:160], in0=lp[:, 384:448], in1=lp[:, 448:512], op=ALU.mult),
           reads=[r_sm], writes=[r_sm])
    rec.op("dve", lambda e: e.reduce_sum(out=sm[:, 16:17], in_=lp[:, 32:96], axis=mybir.AxisListType.X),
           reads=[r_sm], writes=[r_sm])
    rec.op("dve", lambda e: e.reduce_sum(out=sm[:, 17:18], in_=lp[:, 96:160], axis=mybir.AxisListType.X),
           reads=[r_sm], writes=[r_sm])
    rec.op("act", lambda e: e.activation(out=sm[:, 18:20], in_=sm[:, 16:18], func=AF.Exp),
           reads=[r_sm], writes=[r_sm])
    rec.op("dve", lambda e: e.scalar_tensor_tensor(out=sm[:, 0:1], in0=sm[:, 19:20], scalar=-lam_init,
                                                   in1=sm[:, 18:19], op0=ALU.add, op1=ALU.subtract),
           reads=[r_sm], writes=[r_sm])
    sc = VCOLS[("diff_subnorm", l)]
    rec.op("dve", lambda e: e.tensor_scalar(out=sm[:, 1:2], in0=g.vecs[:, sc:sc + 1], scalar1=1.0 - lam_init,
                                            scalar2=None, op0=ALU.mult),
           reads=[r_sm, g.r_const], writes=[r_sm])
    sk = sm[:, 8:16].rearrange("p (a b) -> p a b", b=2)
    rec.op("act", lambda e: e.activation(out=sm[0:64, 2:6], in_=sk[0:64, :, 0], func=AF.Exp),
           reads=[r_sm], writes=[r_sm])
    rec.op("act", lambda e: e.activation(out=sm[64:128, 2:6], in_=sk[64:128, :, 1], func=AF.Exp),
           reads=[r_sm], writes=[r_sm])

    rec.barrier(exclude=("pf",))
    al.off = xoff
    rt = [al.take(KC * T, F32) for i in range(2)]
    sq = al.take(KC * T, BF16)
    rstd = [al.take(T, F32) for i in range(2)]
    r_rt, r_sq, r_rstd = [Res(), Res()], Res(), [Res(), Res()]
    gcol = VCOLS[("mix_norm", l)]
    for t in range(NT):
        b = t % 2
        rec.op("sp", lambda e, t=t, b=b: e.dma_start(out=rt[b].rearrange("p (c n) -> p c n", c=KC),
                                                     in_=Rv[:, :, t * T:(t + 1) * T]),
               reads=[g.r_R[t]], writes=[r_rt[b]], dma="rtld%d" % b)
        norm_tile(g, rt[b], r_rt[b], sq, r_sq, rstd[b], r_rstd[b], 4 + b, uT[:, t * T:], r_uT, gcol, xn_stride=S)
        src = uT.rearrange("p (c n) -> p c n", c=KC)[:, :, t * T:(t + 1) * T]
        rec.op("sp", lambda e, t=t, src=src: e.dma_start(out=XNv[:, :, t * T:(t + 1) * T], in_=src),
               reads=[r_uT], writes=[g.r_XN[t]], dma="xnst")
    rec.barrier(exclude=("pf",))
    al.off = xoff
    memx = al.take(2 * D, F32)
    memnT = al.take(KC * MEM, BF16)
    junk = al.take(D, BF16)
    r_memx, r_memnT, r_junk = Res(), Res(), Res()
    r_mk = Res()
    rec.op("sp", lambda e: e.dma_start(out=memx.rearrange("p (a d) -> p a d", a=2),
                                       in_=g.mem.rearrange("(a p) d -> p a d", p=128)),
           writes=[r_memx], dma="memx")
    for a in range(2):
        rec.op("act", lambda e, a=a: e.activation(out=junk, in_=memx[:, a * D:(a + 1) * D], func=AF.Square,
                                                  accum_out=sm[:, 20 + a:21 + a]),
               reads=[r_memx], writes=[r_junk, r_sm])
    rec.op("act", lambda e: e.activation(out=sm[:, 22:24], in_=sm[:, 20:22], func=AF.Ln, bias=g.epsc[:, 0:1],
                                         scale=1.0 / D),
           reads=[r_sm, g.r_const], writes=[r_sm])
    rec.op("act", lambda e: e.activation(out=sm[:, 22:24], in_=sm[:, 22:24], func=AF.Exp, scale=-0.5),
           reads=[r_sm], writes=[r_sm])
    for a in range(2):
        rec.op("dve", lambda e, a=a: e.tensor_scalar(out=memx[:, a * D:(a + 1) * D], in0=memx[:, a * D:(a + 1) * D],
                                                     scalar1=sm[:, 22 + a:23 + a], scalar2=None, op0=ALU.mult),
               reads=[r_memx, r_sm], writes=[r_memx])
    mcol = VCOLS[("mem_norm", l)]
    for c in range(KC):
        pi = 6 + c % 2
        ps = g.psum[pi]

        def tr(e, c=c, ps=ps):
            ins = None
            for a in range(2):
                ins = e.transpose(out=ps[:, a * 128:(a + 1) * 128],
                                  in_=memx[:, a * D + c * 128: a * D + (c + 1) * 128], identity=g.ident[:])
            return ins
        rec.op("pe", tr, reads=[r_memx, g.r_const], writes=[g.psres[pi]])
        rec.op("dve", lambda e, c=c, ps=ps: e.tensor_scalar(out=memnT[:, c * MEM:(c + 1) * MEM], in0=ps[:, 0:MEM],
                                                            scalar1=g.vecs[:, mcol + c:mcol + c + 1], scalar2=None,
                                                            op0=ALU.mult),
               reads=[g.psres[pi], g.r_const], writes=[r_memnT])
    for hm in range(4):
        pi = 6 + hm % 2
        ps = g.psum[pi]
        rec.op("pe", mm_group(ps[:, 0:MEM], [(wmem_sb[:, c * D + hm * 128: c * D + (hm + 1) * 128],
                                              memnT[:, c * MEM:(c + 1) * MEM]) for c in range(KC)]),
               reads=[r_wmem, r_memnT], writes=[g.psres[pi]])
        rec.op("dve", lambda e, hm=hm, ps=ps: e.tensor_copy(out=g.mkT[:, hm * MEM:(hm + 1) * MEM], in_=ps[:, 0:MEM]),
               reads=[g.psres[pi]], writes=[r_mk])
    for mc in range(2):
        pi = 6 + mc % 2
        ps = g.psum[pi]
        rec.op("pe", mm_group(ps[:, :], [(memnT[:, c * MEM + mc * 128: c * MEM + (mc + 1) * 128],
                                          wmem_sb[:, c * D + 512: c * D + 1024]) for c in range(KC)]),
               reads=[r_wmem, r_memnT], writes=[g.psres[pi]])
        rec.op("dve", lambda e, mc=mc, ps=ps: e.tensor_copy(out=g.mv[:, mc * 512:(mc + 1) * 512], in_=ps[:, :]),
               reads=[g.psres[pi]], writes=[r_mk])

    rec.barrier()
    if getattr(g, "stop_after", "") == "m1u":
        return

    al.off = xoff
    QZe = [al.take(T, BF16) for i in range(4)]
    QZo = [al.take(T, BF16) for i in range(4)]
    QM = al.take(4 * T, BF16)
    KD = [al.take(640, BF16) for i in range(2)]
    VL = [al.take(640, BF16) for i in range(2)]
    VR = [al.take(640, BF16) for i in range(2)]
    Pr = [al.take(T, BF16) for i in range(3)]
    wkd = al.take(KC * 256, BF16)
    osst = [al.take(4 * T, BF16) for i in range(2)]
    omst = [al.take(4 * T, BF16) for i in range(2)]
    tmp = [al.take(T, F32) for i in range(2)]
    r_QZ = [Res() for i in range(4)]
    r_QM, r_KD, r_V, r_wkd = Res(), [Res(), Res()], Res(), Res()
    r_Pr = [Res(), Res(), Res()]
    nit = [0]
    r_os, r_om, r_tmp = [Res(), Res()], [Res(), Res()], [Res(), Res()]
    OSv = g.OS.rearrange("c p n -> p c n")
    OMv = g.OM.rearrange("c p n -> p c n")
    for kv in range(2):
        for half in range(2):
            rec.op("pool", lambda e, kv=kv, half=half: e.dma_start(
                out=wkd.rearrange("p (c n) -> p c n", c=KC)[:, :, kv * 128 + half * 64: kv * 128 + half * 64 + 64],
                in_=wv[:, :, 3584 + kv * 64: 3584 + kv * 64 + 64]), writes=[r_wkd], dma="wkd")
    for cs in range(4):
        rec.op("pool", lambda e, cs=cs: e.memset(QZe[cs][64:128, :], 0.0), writes=[r_QZ[cs]])
        rec.op("pool", lambda e, cs=cs: e.memset(QZo[cs][0:64, :], 0.0), writes=[r_QZ[cs]])
    for kv in range(2):
        rec.op("pool", lambda e, kv=kv: e.memset(VL[kv], 0.0), writes=[r_V])
        rec.op("pool", lambda e, kv=kv: e.memset(VR[kv], 0.0), writes=[r_V])
    PS_O, PS_D, PS_OM, PS_DM = 2, 3, 4, 5
    npj = [0]

    def projbank():
        npj[0] += 1
        return 6 + npj[0] % 2

    nsw = [0]
    nme = [0]
    for t in range(NT):
        tok = t * T
        for cs in range(4):
            pi = projbank()
            ps = g.psum[pi]
            rec.op("pe", mm_group(ps, [(W(c, 3072 + cs * 128), U(c, tok, T)) for c in range(KC)]),
                   reads=[r_win, r_uT], writes=[g.psres[pi]])
            rec.op("dve", lambda e, cs=cs, ps=ps: e.tensor_scalar(out=QZe[cs][0:64, :], in0=ps[0:64, :], scalar1=0.125,
                                                                  scalar2=None, op0=ALU.mult),
                   reads=[g.psres[pi]], writes=[r_QZ[cs]])
            rec.op("act", lambda e, cs=cs, ps=ps: e.mul(out=QZo[cs][64:128, :], in_=ps[64:128, :], mul=0.125),
                   reads=[g.psres[pi]], writes=[r_QZ[cs]])
        for kv in range(2):
            pi = projbank()
            ps = g.psum[pi]
            rec.op("pe", mm_group(ps, [(wkd[:, c * 256 + kv * 128: c * 256 + (kv + 1) * 128], U(c, tok, T))
                                       for c in range(KC)]),
                   reads=[r_wkd, r_uT], writes=[g.psres[pi]])
            rec.op("dve", lambda e, kv=kv, ps=ps: e.tensor_copy(out=KD[kv][:, 128:640], in_=ps),
                   reads=[g.psres[pi]], writes=[r_KD[kv]])
        pi = projbank()
        ps = g.psum[pi]

        def vproj(e, ps=ps, tok=tok):
            ins = None
            for blk in range(4):
                for c in range(KC):
                    ins = e.matmul(ps[:, blk * 128:(blk + 1) * 128], lhsT=U(c, tok + blk * 128, 128),
                                   rhs=W(c, 3712), start=(c == 0), stop=(c == KC - 1))
            return ins
        rec.op("pe", vproj, reads=[r_win, r_uT], writes=[g.psres[pi]])
        ps3 = ps.rearrange("p (b n) -> p b n", b=4)
        for kv in range(2):
            vl3 = VL[kv].rearrange("p (b n) -> p b n", b=5)
            vr3 = VR[kv].rearrange("p (b n) -> p b n", b=5)
            rec.op("dve", lambda e, vl3=vl3, ps3=ps3, kv=kv: e.tensor_copy(out=vl3[:, 1:5, 0:64],
                                                                           in_=ps3[:, :, kv * 64:(kv + 1) * 64]),
                   reads=[g.psres[pi]], writes=[r_V])
            rec.op("act", lambda e, vr3=vr3, ps3=ps3, kv=kv: e.copy(out=vr3[:, 1:5, 64:128],
                                                                    in_=ps3[:, :, kv * 64:(kv + 1) * 64]),
                   reads=[g.psres[pi]], writes=[r_V])
        for hm in range(4):
            pi = projbank()
            ps = g.psum[pi]
            rec.op("pe", mm_group(ps, [(W(c, 3840 + hm * 128), U(c, tok, T)) for c in range(KC)]),
                   reads=[r_win, r_uT], writes=[g.psres[pi]])
            rec.op("act", lambda e, hm=hm, ps=ps: e.copy(out=QM[:, hm * T:(hm + 1) * T], in_=ps),
                   reads=[g.psres[pi]], writes=[r_QM])
        ob = t % 2
        items = [("s", cs, qb) for cs in range(4) for qb in range(4)] + [("m", hm, mc) for hm in range(4) for mc in range(2)]

        def emit_qk(it, i):
            sb_i = i % 2
            Sb = g.psum[sb_i]
            P = Pr[i % 3]
            r_P = r_Pr[i % 3]
            if it[0] == "s":
                _, cs, qb = it
                kv = cs // 2
                n = 4 * t + qb

                def qk(e, cs=cs, kv=kv, qb=qb, n=n, Sb=Sb):
                    ins = None
                    first = True
                    for hh, QZ in enumerate((QZe[cs], QZo[cs])):
                        for w in range(2):
                            if n == 0 and w == 0:
                                continue
                            ins = e.matmul(Sb[:, hh * 256 + w * 128: hh * 256 + (w + 1) * 128],
                                           lhsT=KD[kv][:, (qb + w) * 128:(qb + w + 1) * 128],
                                           rhs=QZ[:, qb * 128:(qb + 1) * 128], start=first, stop=False)
                            first = False
                    mb = C_SWAMB + 2 * cs * 256
                    if n == 0:
                        for hh in range(2):
                            ins = e.matmul(Sb[:, hh * 256 + 128: hh * 256 + 256], lhsT=g.identb,
                                           rhs=g.cb[:, mb + hh * 256 + 128: mb + hh * 256 + 256],
                                           start=False, stop=(hh == 1))
                    else:
                        ins = e.matmul(Sb[:, :], lhsT=g.identb, rhs=g.cb[:, mb: mb + 512], start=False, stop=True)
                    return ins
                rec.op("pe", qk, reads=[r_QZ[cs], r_KD[kv], g.r_const], writes=[g.psres[sb_i]])
                if n == 0:
                    rec.op("act", lambda e, P=P, Sb=Sb: e.activation(
                        out=P.rearrange("p (a b) -> p a b", a=2)[:, :, 128:256],
                        in_=Sb.rearrange("p (a b) -> p a b", a=2)[:, :, 128:256], func=AF.Exp),
                        reads=[g.psres[sb_i]], writes=[r_P])
                else:
                    rec.op("act", lambda e, P=P, Sb=Sb: e.activation(out=P, in_=Sb, func=AF.Exp),
                           reads=[g.psres[sb_i]], writes=[r_P])
            else:
                _, hm, mc = it
                rec.op("pe", mm_group(Sb, [(g.mkT[:, hm * MEM + mc * 128: hm * MEM + (mc + 1) * 128],
                                            QM[:, hm * T:(hm + 1) * T])]),
                       reads=[r_mk, r_QM], writes=[g.psres[sb_i]])
                rec.op("act", lambda e, P=P, Sb=Sb: e.activation(out=P, in_=Sb, func=AF.Exp, scale=128.0 ** -0.5),
                       reads=[g.psres[sb_i]], writes=[r_P])

        def emit_pv(it, i):
            P = Pr[i % 3]
            r_P = r_Pr[i % 3]
            grp = it[1] if it[0] == "s" else 4 + it[1]
            PS_O, PS_D = (2, 3) if grp % 2 == 0 else (4, 5)
            PS_OM, PS_DM = PS_O, PS_D
            if it[0] == "s":
                _, cs, qb = it
                kv = cs // 2
                n = 4 * t + qb

                def pv(e, P=P, kv=kv, qb=qb, n=n, PS_O=PS_O, PS_D=PS_D):
                    ins = None
                    for bank, Lm, Rm in ((PS_O, VL[kv], VR[kv]), (PS_D, None, None)):
                        items2 = []
                        for hh in range(2):
                            for w in range(2):
                                if n == 0 and w == 0:
                                    continue
                                items2.append((hh, w))
                        for ii, (hh, w) in enumerate(items2):
                            if Lm is None:
                                lhs = g.onesL if hh == 0 else g.onesR
                            else:
                                src = Lm if hh == 0 else Rm
                                lhs = src[:, (qb + w) * 128:(qb + w + 1) * 128]
                            ins = e.matmul(g.psum[bank][:, qb * 128:(qb + 1) * 128], lhsT=lhs,
                                           rhs=P[:, hh * 256 + w * 128: hh * 256 + (w + 1) * 128],
                                           start=(ii == 0), stop=(ii == len(items2) - 1))
                    return ins
                rec.op("pe", pv, reads=[r_P, r_V, g.r_const], writes=[g.psres[PS_O], g.psres[PS_D]])
                if qb == 3:
                    tb = cs % 2
                    rec.op("act", lambda e, tb=tb, cs=cs, PS_D=PS_D: e.activation(out=tmp[tb], in_=g.psum[PS_D], func=AF.Ln,
                                                                                 bias=sm[:, 2 + cs:3 + cs]),
                           reads=[g.psres[PS_D], r_sm], writes=[r_tmp[tb]])
                    rec.op("act", lambda e, tb=tb: e.activation(out=tmp[tb], in_=tmp[tb], func=AF.Exp, scale=-1.0),
                           reads=[r_tmp[tb]], writes=[r_tmp[tb]])
                    rec.op("dve", lambda e, tb=tb, cs=cs, ob=ob, PS_O=PS_O: e.tensor_tensor(out=osst[ob][:, cs * T:(cs + 1) * T],
                                                                          in0=g.psum[PS_O], in1=tmp[tb], op=ALU.mult),
                           reads=[g.psres[PS_O], r_tmp[tb]], writes=[r_os[ob]])
            else:
                _, hm, mc = it

                def pvm(e, P=P, hm=hm, mc=mc, PS_OM=PS_OM, PS_DM=PS_DM):
                    e.matmul(g.psum[PS_OM], lhsT=g.mv[:, mc * 512 + hm * 128: mc * 512 + (hm + 1) * 128], rhs=P,
                             start=(mc == 0), stop=(mc == 1))
                    return e.matmul(g.psum[PS_DM], lhsT=g.ones1, rhs=P, start=(mc == 0), stop=(mc == 1))
                rec.op("pe", pvm, reads=[r_P, r_mk, g.r_const], writes=[g.psres[PS_OM], g.psres[PS_DM]])
                if mc == 1:
                    tb = hm % 2
                    rec.op("act", lambda e, tb=tb, PS_DM=PS_DM: e.activation(out=tmp[tb], in_=g.psum[PS_DM], func=AF.Ln),
                           reads=[g.psres[PS_DM]], writes=[r_tmp[tb]])
                    rec.op("act", lambda e, tb=tb: e.activation(out=tmp[tb], in_=tmp[tb], func=AF.Exp, scale=-1.0),
                           reads=[r_tmp[tb]], writes=[r_tmp[tb]])
                    rec.op("dve", lambda e, tb=tb, hm=hm, ob=ob, PS_OM=PS_OM: e.tensor_tensor(out=omst[ob][:, hm * T:(hm + 1) * T],
                                                                          in0=g.psum[PS_OM], in1=tmp[tb], op=ALU.mult),
                           reads=[g.psres[PS_OM], r_tmp[tb]], writes=[r_om[ob]])

        for i, it in enumerate(items):
            emit_qk(it, nit[0] + i)
            if i >= 1:
                emit_pv(items[i - 1], nit[0] + i - 1)
        emit_pv(items[-1], nit[0] + len(items) - 1)
        nit[0] += len(items)
        rec.op("sp", lambda e, ob=ob, tok=tok: e.dma_start(out=OSv[:, :, tok:tok + T],
                                                           in_=osst[ob].rearrange("p (c n) -> p c n", c=4)),
               reads=[r_os[ob]], dma="osst%d" % ob)
        rec.op("sp", lambda e, ob=ob, tok=tok: e.dma_start(out=OMv[:, :, tok:tok + T],
                                                           in_=omst[ob].rearrange("p (c n) -> p c n", c=4)),
               reads=[r_om[ob]], dma="omst%d" % ob)
        if t + 1 < NT:
            for kv in range(2):
                rec.op("pool", lambda e, kv=kv: e.tensor_copy(out=KD[kv][:, 0:128], in_=KD[kv][:, 512:640]),
                       reads=[r_KD[kv]], writes=[r_KD[kv]])
                rec.op("pool", lambda e, kv=kv: e.tensor_copy(out=VL[kv][:, 0:128], in_=VL[kv][:, 512:640]),
                       reads=[r_V], writes=[r_V])
                rec.op("pool", lambda e, kv=kv: e.tensor_copy(out=VR[kv][:, 0:128], in_=VR[kv][:, 512:640]),
                       reads=[r_V], writes=[r_V])
    rec.barrier()
    if getattr(g, "stop_after", "") == "m1a":
        return

    al.off = xoff
    KA = [al.take(S, BF16) for m in range(2)]
    Vh = al.take(S, BF16)
    QA = [[al.take(T, BF16) for m in range(2)] for b in range(2)]
    Pd = [al.take(2 * T, BF16) for b in range(3)]
    rd = [[al.take(T, F32) for i in range(2)] for es in range(2)]
    tt = [[al.take(T, F32) for i in range(2)] for es in range(2)]
    pending = []
    gstep = [0]
    sqb = al.take(T, BF16)
    rstd2 = al.take(T, F32)
    ost = [al.take(T, BF16) for i in range(2)]
    r_KA, r_Vh = Res("KA"), Res("Vh")
    r_QA = [Res("QA0"), Res("QA1")]
    r_Pd = [Res(), Res(), Res()]
    r_rd, r_tt = [[Res(), Res()], [Res(), Res()]], [[Res(), Res()], [Res(), Res()]]
    r_sqb, r_rstd2, r_ost = Res(), Res(), [Res(), Res()]
    ODv = g.OD
    rec.op("pool", lambda e: e.memset(KA[0][64:128, :], 0.0), writes=[r_KA])
    rec.op("pool", lambda e: e.memset(KA[1][0:64, :], 0.0), writes=[r_KA])
    for b in range(2):
        rec.op("pool", lambda e, b=b: e.memset(QA[b][0][64:128, :], 0.0), writes=[r_QA[b]])
        rec.op("pool", lambda e, b=b: e.memset(QA[b][1][0:64, :], 0.0), writes=[r_QA[b]])
        rec.op("pool", lambda e, b=b: e.dma_start(out=QA[b][0][64:66, :], in_=g.qaug_d[:, :]),
               writes=[r_QA[b]], dma="qaug%d" % b)
        rec.op("pool", lambda e, b=b: e.dma_start(out=QA[b][1][0:2, :], in_=g.qaug_d[:, :]),
               writes=[r_QA[b]], dma="qaug%d" % b)
    O_B, D_B = (4, 5), (6, 7)
    Sslot = [g.PS[:, 0:1024], g.PS[:, 1024:2048]]
    r_Sslot = [[g.psres[0], g.psres[1]], [g.psres[2], g.psres[3]]]
    nep = [0]
    nbk = [0]
    nsl = [0]

    def bank4():
        nbk[0] += 1
        return nbk[0] % 4

    def qproj(h, qt):
        b = qt % 2
        pi = bank4()
        ps = g.psum[pi]
        rec.op("pe", mm_group(ps, [(W(c, h * 128), U(c, qt * T, T)) for c in range(KC)]),
               reads=[r_win, r_uT], writes=[g.psres[pi]])
        rec.op("dve", lambda e: e.tensor_scalar(out=QA[b][0][0:64, :], in0=ps[0:64, :], scalar1=0.125,
                                                scalar2=None, op0=ALU.mult),
               reads=[g.psres[pi]], writes=[r_QA[b]])
        rec.op("dve", lambda e: e.tensor_scalar(out=QA[b][1][64:128, :], in0=ps[64:128, :], scalar1=0.125,
                                                scalar2=None, op0=ALU.mult),
               reads=[g.psres[pi]], writes=[r_QA[b]])

    for h in range(8):
        slope = 2.0 ** (-(h + 1))
        rec.op("pool", lambda e, slope=slope: e.memset(KA[0][64:66, :], slope), writes=[r_KA])
        rec.op("pool", lambda e, slope=slope: e.memset(KA[1][0:2, :], slope), writes=[r_KA])
        for t8 in range(NT):
            pi = bank4()
            ps = g.psum[pi]
            rec.op("pe", mm_group(ps, [(W(c, 1024 + h * 128), U(c, t8 * T, T)) for c in range(KC)]),
                   reads=[r_win, r_uT], writes=[g.psres[pi]])
            rec.op("dve", lambda e, ps=ps, t8=t8: e.tensor_copy(out=KA[0][0:64, t8 * T:(t8 + 1) * T], in_=ps[0:64, :]),
                   reads=[g.psres[pi]], writes=[r_KA])
            rec.op("act", lambda e, ps=ps, t8=t8: e.copy(out=KA[1][64:128, t8 * T:(t8 + 1) * T], in_=ps[64:128, :]),
                   reads=[g.psres[pi]], writes=[r_KA])
        for g4 in range(8):
            pi = bank4()
            ps = g.psum[pi]

            def vproj2(e, ps=ps, g4=g4, h=h):
                ins = None
                for blk in range(4):
                    for c in range(KC):
                        ins = e.matmul(ps[:, blk * 128:(blk + 1) * 128], lhsT=U(c, (g4 * 4 + blk) * 128, 128),
                                       rhs=W(c, 2048 + h * 128), start=(c == 0), stop=(c == KC - 1))
                return ins
            rec.op("pe", vproj2, reads=[r_win, r_uT], writes=[g.psres[pi]])
            if g4 % 2 == 0:
                rec.op("dve", lambda e, ps=ps, g4=g4: e.tensor_copy(out=Vh[:, g4 * T:(g4 + 1) * T], in_=ps),
                       reads=[g.psres[pi]], writes=[r_Vh])
            else:
                rec.op("act", lambda e, ps=ps, g4=g4: e.copy(out=Vh[:, g4 * T:(g4 + 1) * T], in_=ps),
                       reads=[g.psres[pi]], writes=[r_Vh])
        qproj(h, 0)
        for qt in range(NT):
            b = qt % 2
            nkb = 4 * (qt + 1)

            def pv_ops(kb):
                pb = kb % 3
                dl = kb - 4 * qt
                c0 = 128 * dl if dl > 0 else 0

                def pv(e, pb=pb, c0=c0, kb=kb):
                    ins = None
                    for m in range(2):
                        e.matmul(g.psum[O_B[m]][:, c0:T], lhsT=Vh[:, kb * 128:(kb + 1) * 128],
                                 rhs=Pd[pb][:, m * T + c0:(m + 1) * T], start=(kb == 0), stop=(kb == nkb - 1))
                        ins = e.matmul(g.psum[D_B[m]][:, c0:T], lhsT=g.ones1, rhs=Pd[pb][:, m * T + c0:(m + 1) * T],
                                       start=(kb == 0), stop=(kb == nkb - 1))
                    return ins
                rec.op("pe", pv, reads=[r_Pd[pb], r_Vh, g.r_const],
                       writes=[g.psres[O_B[0]], g.psres[O_B[1]], g.psres[D_B[0]], g.psres[D_B[1]]])

            for kb in range(nkb):
                pb = kb % 3
                sl = nsl[0] % 2
                nsl[0] += 1
                dl = kb - 4 * qt
                c0 = 128 * dl if dl > 0 else 0

                def qk(e, kb=kb, dl=dl, c0=c0, b=b, sl=sl):
                    ins = None
                    for m in range(2):
                        Sb = Sslot[sl][:, m * T:(m + 1) * T]
                        ins = e.matmul(Sb[:, c0:T], lhsT=KA[m][:, kb * 128:(kb + 1) * 128], rhs=QA[b][m][:, c0:T],
                                       start=True, stop=(dl < 0))
                        if dl >= 0:
                            ins = e.matmul(Sb[:, c0:c0 + 128], lhsT=g.identb, rhs=g.tri, start=False, stop=True)
                    return ins
                rec.op("pe", qk, reads=[r_KA, r_QA[b], g.r_const], writes=r_Sslot[sl])
                bcol = h * 32 + dl + 28
                rec.op("act", lambda e, pb=pb, c0=c0, bcol=bcol, sl=sl: e.activation(
                    out=Pd[pb].rearrange("p (m n) -> p m n", m=2)[:, :, c0:T],
                    in_=Sslot[sl].rearrange("p (m n) -> p m n", m=2)[:, :, c0:T], func=AF.Exp,
                    bias=g.btab[:, bcol:bcol + 1]),
                    reads=r_Sslot[sl] + [g.r_const], writes=[r_Pd[pb]])
                if kb == 0 and qt + 1 < NT:
                    qproj(h, qt + 1)
                gstep[0] += 1
                if pending and gstep[0] >= pending[0][0]:
                    pending.pop(0)[1]()
                if kb >= 2:
                    pv_ops(kb - 2)
            pv_ops(nkb - 2)
            pv_ops(nkb - 1)
            es = nep[0] % 2
            ob = nep[0] % 2
            nep[0] += 1
            for m in range(2):
                rec.op("dve", lambda e, m=m, es=es: e.tensor_copy(out=rd[es][m], in_=g.psum[D_B[m]]),
                       reads=[g.psres[D_B[m]]], writes=[r_rd[es][m]])
            for m in range(2):
                rec.op("dve", lambda e, m=m, es=es: e.tensor_copy(out=tt[es][m], in_=g.psum[O_B[m]]),
                       reads=[g.psres[O_B[m]]], writes=[r_tt[es][m]])

            def part2(es=es, ob=ob, h=h, qt=qt):
                for m in range(2):
                    rec.op("dve", lambda e, m=m: e.reciprocal(out=rd[es][m], in_=rd[es][m]),
                           reads=[r_rd[es][m]], writes=[r_rd[es][m]])
                for m in range(2):
                    rec.op("dve", lambda e, m=m: e.tensor_tensor(out=tt[es][m], in0=tt[es][m], in1=rd[es][m], op=ALU.mult),
                           reads=[r_tt[es][m], r_rd[es][m]], writes=[r_tt[es][m]])
                rec.op("dve", lambda e: e.scalar_tensor_tensor(out=tt[es][0], in0=tt[es][1], scalar=sm[:, 0:1], in1=tt[es][0],
                                                               op0=ALU.mult, op1=ALU.add),
                       reads=[r_tt[es][0], r_tt[es][1], r_sm], writes=[r_tt[es][0]])
                rec.op("pool", lambda e: e.tensor_tensor(out=sqb, in0=tt[es][0], in1=tt[es][0], op=ALU.mult),
                       reads=[r_tt[es][0]], writes=[r_sqb])
                pi = bank4()
                ps = g.psum[pi]
                rec.op("pe", mm_group(ps, [(g.ones128, sqb)]), reads=[r_sqb, g.r_const], writes=[g.psres[pi]])
                rsqrt_ps(g, ps, g.psres[pi], rstd2, r_rstd2)
                rec.op("dve", lambda e: e.scalar_tensor_tensor(out=ost[ob], in0=tt[es][0], scalar=sm[:, 1:2], in1=rstd2,
                                                               op0=ALU.mult, op1=ALU.mult),
                       reads=[r_tt[es][0], r_rstd2, r_sm], writes=[r_ost[ob]])
                rec.op("sp", lambda e: e.dma_start(out=ODv[h, :, qt * T:(qt + 1) * T], in_=ost[ob]),
                       reads=[r_ost[ob]], dma="odst%d" % ob)
            pending.append((gstep[0] + 12, part2))
    while pending:
        pending.pop(0)[1]()
    rec.barrier()


def phase_m2(g, l):
    nc, rec = g.nc, g.rec
    al = Alloc(g.BIG)
    wg = al.take(KC * 3072, BF16)
    wbd = al.take(8 * D, BF16)
    wbs = al.take(4 * D, BF16)
    wbm = al.take(4 * D, BF16)
    wo = al.take(8 * D, BF16)
    woff = al.off
    stages = [al.take(4096, F32) for i in range(5)]
    al.off = woff
    xn = [al.take(KC * T, BF16) for i in range(2)]
    od = [al.take(8 * T, BF16) for i in range(2)]
    osm = [al.take(8 * T, BF16) for i in range(2)]
    rt = [al.take(KC * T, F32) for i in range(2)]
    mg = al.take(KC * T, BF16)
    sg = [al.take(T, F32) for i in range(3)]
    m1 = al.take(T, F32)
    m2 = al.take(T, F32)
    m3 = al.take(T, F32)
    r_w = Res("w")
    r_xn, r_od, r_osm, r_rt = [Res(), Res()], [Res(), Res()], [Res(), Res()], [Res(), Res()]
    r_mg, r_sg, r_m1, r_m2, r_m3 = Res(), [Res(), Res(), Res()], Res(), Res(), Res()
    Rv = g.R.rearrange("c p n -> p c n")
    XNv = g.XN.rearrange("c p n -> p c n")
    ODv = g.OD.rearrange("c p n -> p c n")
    OSv = g.OS.rearrange("c p n -> p c n")
    OMv = g.OM.rearrange("c p n -> p c n")
    wv = g.w_in[l].rearrange("(c p) n -> p c n", p=128)
    pieces = []
    for c in range(KC):
        pieces.append((wg[:, c * 3072:(c + 1) * 3072], wv[:, c, WIN_A:WIN_A + 3072]))
    for dst, src, nk in ((wbd, g.w_br_diff[l], 8), (wbs, g.w_br_swa[l], 4), (wbm, g.w_br_mem[l], 4), (wo, g.w_out[l], 8)):
        sv = src.rearrange("(k p) n -> p k n", p=128)
        for k0 in range(0, nk, 4):
            pieces.append((dst[:, k0 * D:(k0 + 4) * D], sv[:, k0:k0 + 4, :]))
    staged_load(g, pieces, stages, r_w)
    rec.barrier()

    def load(t):
        b = t % 2
        sl = slice(t * T, (t + 1) * T)
        rec.op("sp", lambda e: e.dma_start(out=xn[b].rearrange("p (c n) -> p c n", c=KC), in_=XNv[:, :, sl]),
               reads=[g.r_XN[t]], writes=[r_xn[b]], dma="m2xn%d" % b)
        rec.op("sp", lambda e: e.dma_start(out=od[b].rearrange("p (c n) -> p c n", c=8), in_=ODv[:, :, sl]),
               writes=[r_od[b]], dma="m2od%d" % b)
        rec.op("sp", lambda e: e.dma_start(out=osm[b][:, 0:4 * T].rearrange("p (c n) -> p c n", c=4), in_=OSv[:, :, sl]),
               writes=[r_osm[b]], dma="m2os%d" % b)
        rec.op("sp", lambda e: e.dma_start(out=osm[b][:, 4 * T:8 * T].rearrange("p (c n) -> p c n", c=4), in_=OMv[:, :, sl]),
               writes=[r_osm[b]], dma="m2os%d" % b)
        rec.op("sp", lambda e: e.dma_start(out=rt[b].rearrange("p (c n) -> p c n", c=KC), in_=Rv[:, :, sl]),
               reads=[g.r_R[t]], writes=[r_rt[b]], dma="m2rt%d" % b)

    load(0)
    for t in range(NT):
        b = t % 2
        if t + 1 < NT:
            load(t + 1)
        for oc in range(KC):
            for br in range(3):
                rec.op("pe", mm_group(g.psum[br], [(wg[:, c * 3072 + br * 1024 + oc * 128: c * 3072 + br * 1024 + (oc + 1) * 128],
                                                    xn[b][:, c * T:(c + 1) * T]) for c in range(KC)]),
                       reads=[r_w, r_xn[b]], writes=[g.psres[br]])
                rec.op("act", lambda e, br=br: e.activation(out=sg[br], in_=g.psum[br], func=AF.Sigmoid),
                       reads=[g.psres[br]], writes=[r_sg[br]])
            rec.op("pe", mm_group(g.psum[3], [(wbd[:, k * D + oc * 128: k * D + (oc + 1) * 128], od[b][:, k * T:(k + 1) * T])
                                              for k in range(8)]),
                   reads=[r_w, r_od[b]], writes=[g.psres[3]])
            rec.op("pe", mm_group(g.psum[4], [(wbs[:, k * D + oc * 128: k * D + (oc + 1) * 128], osm[b][:, k * T:(k + 1) * T])
                                              for k in range(4)]),
                   reads=[r_w, r_osm[b]], writes=[g.psres[4]])
            rec.op("pe", mm_group(g.psum[5], [(wbm[:, k * D + oc * 128: k * D + (oc + 1) * 128],
                                               osm[b][:, (4 + k) * T:(5 + k) * T]) for k in range(4)]),
                   reads=[r_w, r_osm[b]], writes=[g.psres[5]])
            rec.op("dve", lambda e: e.tensor_tensor(out=m1, in0=g.psum[3], in1=sg[0], op=ALU.mult),
                   reads=[g.psres[3], r_sg[0]], writes=[r_m1])
            rec.op("dve", lambda e: e.tensor_tensor(out=m2, in0=g.psum[4], in1=sg[1], op=ALU.mult),
                   reads=[g.psres[4], r_sg[1]], writes=[r_m2])
            rec.op("dve", lambda e: e.tensor_tensor(out=m3, in0=g.psum[5], in1=sg[2], op=ALU.mult),
                   reads=[g.psres[5], r_sg[2]], writes=[r_m3])
            rec.op("pool", lambda e: e.tensor_tensor(out=m1, in0=m1, in1=m2, op=ALU.add),
                   reads=[r_m1, r_m2], writes=[r_m1])
            rec.op("pool", lambda e, oc=oc: e.tensor_tensor(out=mg[:, oc * T:(oc + 1) * T], in0=m1, in1=m3, op=ALU.add),
                   reads=[r_m1, r_m3], writes=[r_mg])
        for oc in range(KC):
            pi = 6 + oc % 2
            rec.op("pe", mm_group(g.psum[pi], [(wo[:, k * D + oc * 128: k * D + (oc + 1) * 128], mg[:, k * T:(k + 1) * T])
                                               for k in range(8)]),
                   reads=[r_w, r_mg], writes=[g.psres[pi]])
            ro = rt[b][:, oc * T:(oc + 1) * T]
            rec.op("dve", lambda e, ro=ro, pi=pi: e.tensor_tensor(out=ro, in0=g.psum[pi], in1=ro, op=ALU.add),
                   reads=[g.psres[pi], r_rt[b]], writes=[r_rt[b]])
        rec.op("sp", lambda e, b=b, t=t: e.dma_start(out=Rv[:, :, t * T:(t + 1) * T],
                                                     in_=rt[b].rearrange("p (c n) -> p c n", c=KC)),
               reads=[r_rt[b]], writes=[g.r_R[t]], dma="m2st%d" % b)


def phase_final(g):
    nc, rec = g.nc, g.rec
    al = Alloc(g.BIG)
    rt = [al.take(KC * T, F32) for i in range(2)]
    sq = al.take(KC * T, BF16)
    rstd = al.take(T, F32)
    yt = al.take(KC * T, F32)
    ot = [al.take(4 * D, F32) for i in range(2)]
    r_rt, r_sq, r_rstd, r_yt, r_ot = [Res(), Res()], Res(), Res(), Res(), [Res(), Res()]
    Rv = g.R.rearrange("c p n -> p c n")
    gcol = VCOLS[("final_norm", 0)]
    npj = 0
    for t in range(NT):
        b = t % 2
        rec.op("sp", lambda e, b=b, t=t: e.dma_start(out=rt[b].rearrange("p (c n) -> p c n", c=KC),
                                                     in_=Rv[:, :, t * T:(t + 1) * T]),
               reads=[g.r_R[t]], writes=[r_rt[b]], dma="frt%d" % b)
        norm_tile(g, rt[b], r_rt[b], sq, r_sq, rstd, r_rstd, 7, yt, r_yt, gcol)
        for s4 in range(4):
            for half in range(2):
                pi = npj % 4
                npj += 1
                ps = g.psum[pi]

                def tr(e, ps=ps, s4=s4, half=half):
                    ins = None
                    for cc in range(4):
                        c = half * 4 + cc
                        ins = e.transpose(out=ps[:, cc * 128:(cc + 1) * 128],
                                          in_=yt[:, c * T + s4 * 128: c * T + (s4 + 1) * 128], identity=g.ident[:])
                    return ins
                rec.op("pe", tr, reads=[r_yt, g.r_const], writes=[g.psres[pi]])
                dst = ot[b][:, s4 * D + half * 512: s4 * D + (half + 1) * 512]
                if npj % 2 == 0:
                    rec.op("act", lambda e, dst=dst, ps=ps: e.copy(out=dst, in_=ps), reads=[g.psres[pi]], writes=[r_ot[b]])
                else:
                    rec.op("dve", lambda e, dst=dst, ps=ps: e.tensor_copy(out=dst, in_=ps), reads=[g.psres[pi]], writes=[r_ot[b]])
        dsto = g.out[t * T:(t + 1) * T, :].rearrange("(s p) d -> p s d", p=128)
        rec.op("sp", lambda e, b=b, dsto=dsto: e.dma_start(out=dsto, in_=ot[b].rearrange("p (s d) -> p s d", s=4)),
               reads=[r_ot[b]], dma="fout%d" % b)


def make_vecs(inp):
    v = np.zeros((128, NV), np.float32)
    for l in range(DEPTH):
        for name in ("ffn1_norm", "mix_norm", "ffn2_norm", "mem_norm"):
            c = VCOLS[(name, l)]
            v[:, c:c + KC] = np.asarray(inp[name][l]).reshape(KC, 128).T
        v[:, VCOLS[("diff_subnorm", l)]] = np.asarray(inp["diff_subnorm"][l])
    c = VCOLS[("final_norm", 0)]
    v[:, c:c + KC] = np.asarray(inp["final_norm"]).reshape(KC, 128).T
    return v


def make_consts():
    c = np.zeros((128, NC16), np.float32)
    j = np.arange(128)[:, None].astype(np.float64)
    i = np.arange(128)[None, :].astype(np.float64)
    c[:, C_IDENT:C_IDENT + 128] = np.eye(128)
    c[:, C_TRI:C_TRI + 128] = np.where(j > i, NEG, 0.0)
    for h in range(8):
        sl = 2.0 ** (-(h + 1))
        prev = np.where(j > i, -sl * (i + 128 - j), NEG)
        cur = np.where(j <= i, -sl * (i - j), NEG)
        c[:, C_SWAMB + h * 256: C_SWAMB + h * 256 + 128] = prev
        c[:, C_SWAMB + h * 256 + 128: C_SWAMB + (h + 1) * 256] = cur
    c[:, C_ONESL:C_ONESL + 64] = 1.0
    c[:, C_ONESR + 64:C_ONESR + 128] = 1.0
    c[:, C_ONES128:C_ONES128 + 128] = 1.0 / 128
    c[:, C_ONES1:C_ONES1 + 128] = 1.0
    il = np.arange(T)
    qaug = np.stack([-(il - il % 2), -(il % 2)]).astype(np.float32)
    bt = np.zeros((128, 256), np.float32)
    for h in range(8):
        sl = 2.0 ** (-(h + 1))
        for di in range(32):
            bt[:, h * 32 + di] = sl * (np.arange(128) + 128 * (di - 28))
    return c, qaug, bt


def make_in_maps(inp):
    f = lambda a: np.ascontiguousarray(np.asarray(a, dtype=np.float32))
    shared = {
        "ffn1_wi": f(inp["ffn1_wi"]), "ffn2_wi": f(inp["ffn2_wi"]),
        "ffn1_wo": f(inp["ffn1_wo"]), "ffn2_wo": f(inp["ffn2_wo"]),
        "w_in": f(inp["w_in"]), "w_mem_kv": f(inp["w_mem_kv"]),
        "w_br_diff": f(inp["w_br_diff"]), "w_br_swa": f(inp["w_br_swa"]),
        "w_br_mem": f(inp["w_br_mem"]), "w_out": f(inp["w_out"]),
        "vecs": make_vecs(inp),
        "ident": np.eye(128, dtype=np.float32),
        "diff_lambda": f(inp["diff_lambda"]).reshape(DEPTH, 256),
        "swa_sinks": f(inp["swa_sinks"]),
    }
    shared["cf32"], shared["qaug"], shared["btab"] = make_consts()
    maps = []
    for b in range(NB):
        m = dict(shared)
        m["x"] = f(inp["x"][b])
        m["mem"] = f(inp["mem"][b])
        maps.append(m)
    return maps


def kernel(**inp):
    nc = build()
    res = run_bass_kernel_spmd(nc, make_in_maps(inp), core_ids=list(range(NB)))
    return np.stack([np.asarray(r["out"]) for r in res.results], axis=0).astype(np.float32)
```
